# Optimizing a Trainium2 kernel written in Bass

```python
import jax, jax.numpy as jnp
from jax import lax
import numpy as np

D_MODEL = 1024
BATCH = 8
SEQ = 8192
DEPTH = 1
DEC_BATCH = 32
DEC_SEQ = 16
PAST_LEN = 4096

CHUNK = 64
LEFT_CHUNKS = 8
HEAD_DIM = 64
H_A = 4
H_B = 8
H_M = 4
N_MEM = 256
REL_CLIP = 256
SB_BLOCK = 128
D_FF = 2816
CONV_W = 3
W_A = H_A * HEAD_DIM
W_B = H_B * HEAD_DIM
W_M = H_M * HEAD_DIM
D_IN = 3 * W_A + 3 * W_B + W_M
IN_SPLITS = [W_A, 2 * W_A, 3 * W_A, 3 * W_A + W_B, 3 * W_A + 2 * W_B, 3 * W_A + 3 * W_B]
DN_ALPHA = (2 * DEPTH) ** 0.25
DN_BETA = (8 * DEPTH) ** -0.25
LN_EPS = 1e-5
NEG_INF = -1e30

kernel_name = "chunk_stream_hybrid_encoder_step"


def layer_norm(x, g, b):
    xf = x.astype(jnp.float32)
    mu = jnp.mean(xf, axis=-1, keepdims=True)
    var = jnp.mean(jnp.square(xf - mu), axis=-1, keepdims=True)
    y = (xf - mu) * lax.rsqrt(var + LN_EPS) * g.astype(jnp.float32) + b.astype(jnp.float32)
    return y.astype(x.dtype)


def heads(t, h):
    return t.reshape(t.shape[:-1] + (h, HEAD_DIM))


def in_projection(x, w_in):
    qa, ka, va, qb, kb, vb, qm = jnp.split(x @ w_in, IN_SPLITS, axis=-1)
    return (heads(qa, H_A), heads(ka, H_A), heads(va, H_A),
            heads(qb, H_B), heads(kb, H_B), heads(vb, H_B), heads(qm, H_M))


def memory_kv(mem, w_mem_kv):
    mk, mv = jnp.split(mem @ w_mem_kv, 2, axis=-1)
    return heads(mk, H_M), heads(mv, H_M)


def rel_bias_lookup(table, rel):
    idx = jnp.clip(rel, -REL_CLIP, REL_CLIP) + REL_CLIP
    return table[:, idx].astype(jnp.float32)


def chunk_band_attention(q, k, v, bias):
    s = jnp.einsum('bnqhd,bnkhd->bnhqk', q, k).astype(jnp.float32) * (HEAD_DIM ** -0.5) + bias[None]
    p = jax.nn.softmax(s, axis=-1).astype(v.dtype)
    return jnp.einsum('bnhqk,bnkhd->bnqhd', p, v)


def chunk_attention_prompt(q, k, v, table):
    B, S = q.shape[0], q.shape[1]
    nc = S // CHUNK
    band = (LEFT_CHUNKS + 1) * CHUNK
    qc = q.reshape(B, nc, CHUNK, H_A, HEAD_DIM)

    def gather_band(t):
        tc = t.reshape(B, nc, CHUNK, H_A, HEAD_DIM)
        tp = jnp.pad(tc, ((0, 0), (LEFT_CHUNKS, 0), (0, 0), (0, 0), (0, 0)))
        return jnp.concatenate([tp[:, j:j + nc] for j in range(LEFT_CHUNKS + 1)], axis=2)

    kb, vb = gather_band(k), gather_band(v)
    rel = LEFT_CHUNKS * CHUNK + jnp.arange(CHUNK)[:, None] - jnp.arange(band)[None, :]
    bias = rel_bias_lookup(table, rel)
    key_chunk = jnp.arange(nc)[:, None] - LEFT_CHUNKS + jnp.arange(band)[None, :] // CHUNK
    mask_add = jnp.where(key_chunk >= 0, 0.0, NEG_INF).astype(jnp.float32)
    o = chunk_band_attention(qc, kb, vb, bias[None] + mask_add[:, None, None, :])
    return o.reshape(B, S, W_A)


def chunk_attention_sample(q, k_new, v_new, ck, cv, table, past_len):
    B, T = q.shape[0], q.shape[1]
    C = ck.shape[1]
    kk = jnp.concatenate([ck, k_new], axis=1)
    vv = jnp.concatenate([cv, v_new], axis=1)
    q_pos = past_len + jnp.arange(T)
    k_pos = jnp.concatenate([past_len - C + jnp.arange(C), past_len + jnp.arange(T)])
    bias = rel_bias_lookup(table, q_pos[:, None] - k_pos[None, :])[None]
    o = chunk_band_attention(q[:, None], kk[:, None], vv[:, None], bias)
    return o.reshape(B, T, W_A)


def stick_breaking_attention(q, k, v, q_pos, k_pos):
    z = jnp.einsum('bqhd,bkhd->bhqk', q, k).astype(jnp.float32) * (HEAD_DIM ** -0.5)
    mask = (k_pos[None, :] < q_pos[:, None])[None, None]
    log_stay = jnp.where(mask, jax.nn.log_sigmoid(-z), 0.0)
    later = lax.cumsum(log_stay, axis=3, reverse=True) - log_stay
    w = jnp.where(mask, jnp.exp(jax.nn.log_sigmoid(z) + later), 0.0)
    return jnp.einsum('bhqk,bkhd->bqhd', w.astype(v.dtype), v)


def stick_breaking_prompt(q, k, v):
    B, S = q.shape[0], q.shape[1]
    nb = S // SB_BLOCK
    pos = jnp.arange(S, dtype=jnp.int32)
    qb = q.reshape(B, nb, SB_BLOCK, H_B, HEAD_DIM).transpose(1, 0, 2, 3, 4)
    pb = pos.reshape(nb, SB_BLOCK)
    o = lax.map(lambda a: stick_breaking_attention(a[0], k, v, a[1], pos), (qb, pb))
    return o.transpose(1, 0, 2, 3, 4).reshape(B, S, W_B)


def stick_breaking_sample(q, k_new, v_new, ck, cv, past_len):
    T = q.shape[1]
    kk = jnp.concatenate([ck, k_new], axis=1)
    vv = jnp.concatenate([cv, v_new], axis=1)
    k_pos = jnp.arange(past_len + T, dtype=jnp.int32)
    q_pos = past_len + jnp.arange(T, dtype=jnp.int32)
    o = stick_breaking_attention(q, kk, vv, q_pos, k_pos)
    return o.reshape(q.shape[0], T, W_B)


def memory_attention(q, mk, mv):
    s = jnp.einsum('bqhd,bmhd->bhqm', q, mk).astype(jnp.float32) * (HEAD_DIM ** -0.5)
    p = jax.nn.softmax(s, axis=-1).astype(mv.dtype)
    o = jnp.einsum('bhqm,bmhd->bqhd', p, mv)
    return o.reshape(q.shape[0], q.shape[1], W_M)


def merge_branches(x, oa, ob, om, w_pa, w_pb, w_pm, w_gate, b_gate, w_o):
    g = jax.nn.sigmoid((x @ w_gate + b_gate).astype(jnp.float32)).astype(x.dtype)
    ga, gb, gm = jnp.split(g, 3, axis=-1)
    h = ga * (oa @ w_pa) + gb * (ob @ w_pb) + gm * (om @ w_pm)
    return h @ w_o


def conv_ffn_tail(ext, conv_w, conv_b, w_down):
    T = ext.shape[1] - (CONV_W - 1)
    c = sum(ext[:, j:j + T] * conv_w[j] for j in range(CONV_W)) + conv_b
    gate, val = jnp.split(c, 2, axis=-1)
    return (jax.nn.silu(gate) * val) @ w_down


def setup_inputs(seed: int = 0) -> dict:
    key = jax.random.key(seed)
    ks = jax.random.split(key, 32)
    nrm = jax.random.normal
    f32 = jnp.float32
    a_cache = min(LEFT_CHUNKS * CHUNK, PAST_LEN)
    return {
        "x_prompt": nrm(ks[0], (BATCH, SEQ, D_MODEL), f32),
        "x_sample": nrm(ks[1], (DEC_BATCH, DEC_SEQ, D_MODEL), f32),
        "cache_a_k": nrm(ks[2], (DEPTH, DEC_BATCH, a_cache, H_A, HEAD_DIM), f32),
        "cache_a_v": nrm(ks[3], (DEPTH, DEC_BATCH, a_cache, H_A, HEAD_DIM), f32),
        "cache_b_k": nrm(ks[4], (DEPTH, DEC_BATCH, PAST_LEN, H_B, HEAD_DIM), f32),
        "cache_b_v": nrm(ks[5], (DEPTH, DEC_BATCH, PAST_LEN, H_B, HEAD_DIM), f32),
        "cache_mem_k": nrm(ks[6], (DEPTH, DEC_BATCH, N_MEM, H_M, HEAD_DIM), f32),
        "cache_mem_v": nrm(ks[7], (DEPTH, DEC_BATCH, N_MEM, H_M, HEAD_DIM), f32),
        "state_ffn_conv": nrm(ks[8], (DEPTH, DEC_BATCH, CONV_W - 1, 2 * D_FF), f32),
        "mem_prompt": nrm(ks[9], (BATCH, N_MEM, D_MODEL), f32),
        "w_in": nrm(ks[10], (DEPTH, D_MODEL, D_IN), f32) * D_MODEL ** -0.5,
        "rel_bias": 0.1 * nrm(ks[11], (DEPTH, H_A, 2 * REL_CLIP + 1), f32),
        "w_mem_kv": nrm(ks[12], (DEPTH, D_MODEL, 2 * W_M), f32) * D_MODEL ** -0.5,
        "w_pa": nrm(ks[13], (DEPTH, W_A, D_MODEL), f32) * W_A ** -0.5,
        "w_pb": nrm(ks[14], (DEPTH, W_B, D_MODEL), f32) * W_B ** -0.5,
        "w_pm": nrm(ks[15], (DEPTH, W_M, D_MODEL), f32) * W_M ** -0.5,
        "w_gate": nrm(ks[16], (DEPTH, D_MODEL, 3 * D_MODEL), f32) * D_MODEL ** -0.5,
        "b_gate": 0.01 * nrm(ks[17], (DEPTH, 3 * D_MODEL), f32),
        "w_o": nrm(ks[18], (DEPTH, D_MODEL, D_MODEL), f32) * (D_MODEL ** -0.5 * DN_BETA),
        "ln1_g": 1.0 + 0.01 * nrm(ks[19], (DEPTH, D_MODEL), f32),
        "ln1_b": 0.01 * nrm(ks[20], (DEPTH, D_MODEL), f32),
        "w_up": nrm(ks[21], (DEPTH, D_MODEL, 2 * D_FF), f32) * D_MODEL ** -0.5,
        "conv_w": nrm(ks[22], (DEPTH, CONV_W, 2 * D_FF), f32) * CONV_W ** -0.5,
        "conv_b": 0.01 * nrm(ks[23], (DEPTH, 2 * D_FF), f32),
        "w_down": nrm(ks[24], (DEPTH, D_FF, D_MODEL), f32) * (D_FF ** -0.5 * DN_BETA),
        "ln2_g": 1.0 + 0.01 * nrm(ks[25], (DEPTH, D_MODEL), f32),
        "ln2_b": 0.01 * nrm(ks[26], (DEPTH, D_MODEL), f32),
    }


def reference(x_prompt, x_sample, cache_a_k, cache_a_v, cache_b_k, cache_b_v, cache_mem_k, cache_mem_v,
              state_ffn_conv, mem_prompt, w_in, rel_bias, w_mem_kv, w_pa, w_pb, w_pm, w_gate, b_gate, w_o,
              ln1_g, ln1_b, w_up, conv_w, conv_b, w_down, ln2_g, ln2_b):
    past_len = cache_b_k.shape[2]
    a_keep = min(LEFT_CHUNKS * CHUNK, x_prompt.shape[1])
    yp, ys = x_prompt, x_sample
    pak, pav, pbk, pbv, pmk, pmv, pconv = [], [], [], [], [], [], []
    sak, sav, sbk, sbv, sconv = [], [], [], [], []
    for l in range(DEPTH):
        qa, ka, va, qb, kb, vb, qm = in_projection(yp, w_in[l])
        mk, mv = memory_kv(mem_prompt, w_mem_kv[l])
        oa = chunk_attention_prompt(qa, ka, va, rel_bias[l])
        ob = stick_breaking_prompt(qb, kb, vb)
        om = memory_attention(qm, mk, mv)
        mix = merge_branches(yp, oa, ob, om, w_pa[l], w_pb[l], w_pm[l], w_gate[l], b_gate[l], w_o[l])
        h = layer_norm(DN_ALPHA * yp + mix, ln1_g[l], ln1_b[l])
        u = h @ w_up[l]
        ext = jnp.pad(u, ((0, 0), (CONV_W - 1, 0), (0, 0)))
        yp = layer_norm(DN_ALPHA * h + conv_ffn_tail(ext, conv_w[l], conv_b[l], w_down[l]), ln2_g[l], ln2_b[l])
        pak.append(ka[:, -a_keep:]); pav.append(va[:, -a_keep:])
        pbk.append(kb); pbv.append(vb)
        pmk.append(mk); pmv.append(mv)
        pconv.append(u[:, -(CONV_W - 1):])
        qa, ka, va, qb, kb, vb, qm = in_projection(ys, w_in[l])
        oa = chunk_attention_sample(qa, ka, va, cache_a_k[l], cache_a_v[l], rel_bias[l], past_len)
        ob = stick_breaking_sample(qb, kb, vb, cache_b_k[l], cache_b_v[l], past_len)
        om = memory_attention(qm, cache_mem_k[l], cache_mem_v[l])
        mix = merge_branches(ys, oa, ob, om, w_pa[l], w_pb[l], w_pm[l], w_gate[l], b_gate[l], w_o[l])
        h = layer_norm(DN_ALPHA * ys + mix, ln1_g[l], ln1_b[l])
        u = h @ w_up[l]
        ext = jnp.concatenate([state_ffn_conv[l].astype(u.dtype), u], axis=1)
        ys = layer_norm(DN_ALPHA * h + conv_ffn_tail(ext, conv_w[l], conv_b[l], w_down[l]), ln2_g[l], ln2_b[l])
        sak.append(ka); sav.append(va)
        sbk.append(kb); sbv.append(vb)
        sconv.append(ext[:, -(CONV_W - 1):])
    return (yp, ys,
            jnp.stack(pak), jnp.stack(pav), jnp.stack(pbk), jnp.stack(pbv),
            jnp.stack(pmk), jnp.stack(pmv), jnp.stack(pconv),
            jnp.stack(sak), jnp.stack(sav), jnp.stack(sbk), jnp.stack(sbv), jnp.stack(sconv))
```

```python
import os
import numpy as np
from contextlib import ExitStack
import concourse.bass as bass
import concourse.mybir as mybir
from concourse.bass_utils import run_bass_kernel_spmd

F32 = mybir.dt.float32
BF16 = mybir.dt.bfloat16
AF = mybir.ActivationFunctionType
ALU = mybir.AluOpType
AX = mybir.AxisListType

S = 8192
D = 1024
KD = 8
NBLK = 64
DFF = 2816
NFT = 22
NS = 64
PAST = 4096
NKS = PAST + 16
ALPHA = 2.0 ** 0.25
EPS = 1e-5
NDMA = 40
STOP = int(os.environ.get("MK_STOP", "99"))
LIMIT = int(os.environ.get("MK_LIMIT", "1000000000"))
DBG = int(os.environ.get("MK_DBG", "3"))
DBG_NTOK = int(os.environ.get("MK_NTOK", str(S)))


class Buf:
    __slots__ = ("w", "r")

    def __init__(self):
        self.w = None
        self.r = {}


class Sched:
    def __init__(self, nc, es):
        self.nc = nc
        self.eng = {"pe": nc.tensor, "act": nc.scalar, "dve": nc.vector, "pool": nc.gpsimd, "sp": nc.sync}
        self.semobj = {}
        self.cnt = {}
        for e in ("pe", "act", "dve", "pool"):
            self.semobj[e] = es.enter_context(nc.semaphore("s_" + e))
            self.cnt[e] = 0
        self.dma_tgt = [0] * NDMA
        for k in range(NDMA):
            self.semobj[("d", k)] = es.enter_context(nc.semaphore("s_d%d" % k))
        self.rr = 0
        self.seen = {e: {} for e in self.eng}

    def _need(self, reads, writes):
        need = {}
        for b in reads:
            if b.w is not None:
                k, v = b.w
                if need.get(k, 0) < v:
                    need[k] = v
        for b in writes:
            if b.w is not None:
                k, v = b.w
                if need.get(k, 0) < v:
                    need[k] = v
            for k, v in b.r.items():
                if need.get(k, 0) < v:
                    need[k] = v
        return need

    def _waits(self, e, need):
        seen = self.seen[e]
        for k, v in need.items():
            if k == e and e == "pe":
                continue
            if seen.get(k, 0) < v:
                self.eng[e].wait_ge(self.semobj[k], v)
                seen[k] = v

    def _mark(self, tok, reads, writes):
        k, v = tok
        for b in reads:
            if b.r.get(k, 0) < v:
                b.r[k] = v
        for b in writes:
            b.w = tok
            b.r = {}

    def op(self, e, fn, reads=(), writes=()):
        self.nops = getattr(self, "nops", 0) + 1
        if self.nops > LIMIT:
            return
        if self.nops == LIMIT:
            print("LAST OP", e, reads, writes)
        self._waits(e, self._need(reads, writes))
        inst = fn(self.eng[e])
        self.cnt[e] += 1
        inst.then_inc(self.semobj[e], 1)
        self._mark((e, self.cnt[e]), reads, writes)

    def dma(self, q, out, in_, reads=(), writes=()):
        self.nops = getattr(self, "nops", 0) + 1
        if self.nops > LIMIT:
            return
        if self.nops == LIMIT:
            print("LAST DMA", q, out, in_)
        need = self._need(reads, writes)
        k = self.rr
        self.rr = (k + 1) % NDMA
        key = ("d", k)
        if self.dma_tgt[k] > 0 and need.get(key, 0) < self.dma_tgt[k]:
            need[key] = self.dma_tgt[k]
        self._waits(q, need)
        inst = self.eng[q].dma_start(out=out, in_=in_)
        self.dma_tgt[k] += 16
        inst.then_inc(self.semobj[key], 16)
        self._mark((key, self.dma_tgt[k]), reads, writes)

    def barrier(self):
        for e in self.eng:
            need = {}
            for o in ("pe", "act", "dve", "pool"):
                if o != e and self.cnt[o] > 0:
                    need[o] = self.cnt[o]
            for k in range(NDMA):
                if self.dma_tgt[k] > 0:
                    need[("d", k)] = self.dma_tgt[k]
            seen = self.seen[e]
            for k, v in need.items():
                if seen.get(k, 0) < v:
                    self.eng[e].wait_ge(self.semobj[k], v)
                    seen[k] = v


def build():
    nc = bass.Bass("TRN2", target_bir_lowering=False)

    def din(name, shape, dt=F32):
        return nc.dram_tensor(name, list(shape), dt, kind="ExternalInput").ap()

    def dout(name, shape, dt=F32):
        return nc.dram_tensor(name, list(shape), dt, kind="ExternalOutput").ap()

    def dscr(name, shape, dt=BF16):
        return nc.dram_tensor(name, list(shape), dt, kind="Internal").ap()

    xrT = din("xrT", [D, S + 1])
    xr = din("xr", [S, D])
    memT = din("memT", [D, 256])
    xsT = din("xsT", [D, NS])
    xs = din("xs", [NS, D])
    cak = din("cak", [4, 512, 256])
    cav = din("cav", [4, 512, 256])
    cbk = din("cbk", [4, PAST, 512])
    cbv = din("cbv", [4, PAST, 512])
    cmk = din("cmk", [4, 256, 256])
    cmv = din("cmv", [4, 256, 256])
    stT = din("stT", [128, 2 * NFT, 4, 2])
    bias_p = din("bias_p", [4, 128, 640])
    bias_s = din("bias_s", [4, 16, 528])
    w_in = din("w_in", [D, 2560])
    w_mem = din("w_mem", [D, 512])
    w_pa = din("w_pa", [256, D])
    w_pb = din("w_pb", [512, D])
    w_pm = din("w_pm", [256, D])
    w_gate = din("w_gate", [D, 3 * D])
    b_gateT = din("b_gateT", [128, 24])
    w_o = din("w_o", [D, D])
    ln1_g = din("ln1_g", [128, D])
    ln1_b = din("ln1_b", [128, D])
    w_up = din("w_up", [D, 2 * DFF])
    conv_wT = din("conv_wT", [128, 2 * NFT, 3])
    conv_bT = din("conv_bT", [128, 2 * NFT])
    w_down = din("w_down", [DFF, D])
    ln2_g = din("ln2_g", [128, D])
    ln2_b = din("ln2_b", [128, D])

    yp_d = dout("yp_d", [S, D])
    ys_d = dout("ys_d", [NS, D])
    pak_d = dout("pak_d", [512, 256])
    pav_d = dout("pav_d", [512, 256])
    pbk_d = dout("pbk_d", [S, 512])
    pbv_d = dout("pbv_d", [S + 1, 512])
    pmkv_d = dout("pmkv_d", [256, 512])
    pconv_d = dout("pconv_d", [128, 2 * NFT, 2])
    sak_d = dout("sak_d", [NS, 256])
    sav_d = dout("sav_d", [NS, 256])
    sbk_d = dout("sbk_d", [NS, 512])
    sbv_d = dout("sbv_d", [NS, 512])
    sconv_d = dout("sconv_d", [128, 2 * NFT, 4, 2])

    projT_p = dscr("projT_p", [1792, S])
    projT_s = dscr("projT_s", [1792, NS])
    va_p = dscr("va_p", [S, 256])
    va_s = dscr("va_s", [NS, 256])
    xbf_p = dscr("xbf_p", [D, S])
    xbf_s = dscr("xbf_s", [D, NS])
    mkT_d = dscr("mkT_d", [256, 256])
    mv_d = dscr("mv_d", [256, 256])
    oT_p = dscr("oT_p", [1024, S])
    oT_s = dscr("oT_s", [1024, NS])
    h1_p = dscr("h1_p", [S, D], F32)
    h1_s = dscr("h1_s", [NS, D], F32)
    h1T_p = dscr("h1T_p", [D, S + 2])
    h1T_s = dscr("h1T_s", [D, NS])

    es = ExitStack()
    with es:
        sc = Sched(nc, es)

        uniq = [0]

        def sbt(stack, name, shape, dt):
            uniq[0] += 1
            return stack.enter_context(nc.sbuf_tensor("%s_%d" % (name, uniq[0]), list(shape), dt))

        psA = es.enter_context(nc.psum_tensor("psA", [128, 1024], F32))
        psB = es.enter_context(nc.psum_tensor("psB", [128, 1024], F32))
        psC = es.enter_context(nc.psum_tensor("psC", [128, 512], F32))
        psD = es.enter_context(nc.psum_tensor("psD", [128, 512], F32))
        psT0 = es.enter_context(nc.psum_tensor("psT0", [128, 1024], BF16))
        psT1 = es.enter_context(nc.psum_tensor("psT1", [128, 1024], BF16))
        banks = [(psA, 0), (psA, 512), (psB, 0), (psB, 512), (psC, 0), (psD, 0)]
        bank_b = [Buf() for _ in range(6)]
        psT = [psT0, psT1]
        psT_b = [Buf(), Buf()]

        def bk(i, n=512, parts=128):
            t, o = banks[i]
            return t[0:parts, o:o + n]

        ident = sbt(es, "ident", [128, 128], BF16)
        identf = sbt(es, "identf", [128, 128], F32)
        zeros = sbt(es, "zeros", [128, 512], F32)
        trimask = sbt(es, "trimask", [128, 512], F32)
        lowmask = sbt(es, "lowmask", [128, 128], F32)
        cbuf = Buf()
        sc.op("pool", lambda e: e.memset(zeros[:], 0.0), writes=[cbuf])
        sc.op("pool", lambda e: e.memset(identf[:], 1.0), writes=[cbuf])
        sc.op("pool", lambda e: e.affine_select(out=identf[:], in_=identf[:], pattern=[[-1, 128]], compare_op=ALU.is_equal,
                                                fill=0.0, base=0, channel_multiplier=1), reads=[cbuf], writes=[cbuf])
        sc.op("pool", lambda e: e.tensor_copy(out=ident[:], in_=identf[:]), reads=[cbuf], writes=[cbuf])
        sc.op("pool", lambda e: e.memset(trimask[:], 0.0), writes=[cbuf])
        sc.op("pool", lambda e: e.memset(trimask[:, 0:128], 1.0), reads=[cbuf], writes=[cbuf])
        sc.op("pool", lambda e: e.affine_select(out=trimask[:, 0:128], in_=trimask[:, 0:128], pattern=[[-1, 128]],
                                                compare_op=ALU.is_ge, fill=0.0, base=0, channel_multiplier=1),
              reads=[cbuf], writes=[cbuf])
        sc.op("pool", lambda e: e.memset(lowmask[:], 1.0), writes=[cbuf])
        sc.op("pool", lambda e: e.affine_select(out=lowmask[:], in_=lowmask[:], pattern=[[-1, 128]],
                                                compare_op=ALU.is_gt, fill=0.0, base=0, channel_multiplier=1),
              reads=[cbuf], writes=[cbuf])
        sc.barrier()

        evac_rr = [0]

        def evac(out, in_, reads, writes, scale=None, eng=None):
            writes = list(writes) + list(reads)
            if eng is None:
                eng = "act" if evac_rr[0] % 2 == 0 else "dve"
                evac_rr[0] += 1
            if eng == "act":
                if scale is None:
                    sc.op("act", lambda e: e.copy(out=out, in_=in_), reads, writes)
                else:
                    sc.op("act", lambda e: e.mul(out, in_, float(scale)), reads, writes)
            else:
                if scale is None:
                    sc.op("dve", lambda e: e.tensor_copy(out=out, in_=in_), reads, writes)
                else:
                    sc.op("dve", lambda e: e.tensor_scalar(out=out, in0=in_, scalar1=float(scale), scalar2=None,
                                                           op0=ALU.mult), reads, writes)

        def load_weight_bf(stack, name, w_ap, rows, cols, stage, stage_b):
            nk = rows // 128
            wt = sbt(stack, name, [128, nk, cols], BF16)
            wb = Buf()
            cw = stage[0].shape[1]
            i = 0
            for k in range(nk):
                for c0 in range(0, cols, cw):
                    c1 = min(cols, c0 + cw)
                    st, stb = stage[i % len(stage)], stage_b[i % len(stage)]
                    sc.dma("sp", st[:, 0:c1 - c0], w_ap[k * 128:(k + 1) * 128, c0:c1], writes=[stb])
                    ce = ("pool", "dve", "act")[i % 3]
                    if ce == "act":
                        sc.op(ce, lambda e, st=st, k=k, c0=c0, c1=c1: e.copy(out=wt[:, k, c0:c1], in_=st[:, 0:c1 - c0]),
                              reads=[stb], writes=[wb])
                    else:
                        sc.op(ce, lambda e, st=st, k=k, c0=c0, c1=c1: e.tensor_copy(out=wt[:, k, c0:c1], in_=st[:, 0:c1 - c0]),
                              reads=[stb], writes=[wb])
                    i += 1
            return wt, wb

        def phase1():
            with ExitStack() as ps:
                stage = [sbt(ps, "wst%d" % i, [128, 2560], F32) for i in range(2)]
                stage_b = [Buf(), Buf()]
                win, win_b = load_weight_bf(ps, "win", w_in, D, 2560, stage, stage_b)
                wmem, wmem_b = load_weight_bf(ps, "wmem", w_mem, D, 512, stage, stage_b)
                xf = [sbt(ps, "xf%d" % i, [128, KD, 512], F32) for i in range(2)]
                xf_b = [Buf(), Buf()]
                xb = [sbt(ps, "xb%d" % i, [128, KD, 512], BF16) for i in range(2)]
                xb_b = [Buf(), Buf()]
                fm = [sbt(ps, "fm%d" % i, [128, 14, 512], BF16) for i in range(2)]
                fm_b = [Buf(), Buf()]
                tkv = [sbt(ps, "tkv%d" % i, [128, 2, 512], F32) for i in range(2)]
                tkv_b = [Buf(), Buf()]
                tav = [sbt(ps, "tav%d" % i, [128, 256], BF16) for i in range(2)]
                tav_b = [Buf(), Buf()]
                tka = [sbt(ps, "tka%d" % i, [128, 512], F32) for i in range(2)]
                tka_b = [Buf(), Buf()]
                zrow = sbt(ps, "zrow", [1, 512], F32)
                zb = Buf()
                sc.op("pool", lambda e: e.memset(zrow[:], 0.0), writes=[zb])
                sc.dma("sp", pbv_d[S:S + 1, :], zrow[:], reads=[zb])

                fm_cols = [0, 128, 256, 384, 768, 896, 1024, 1152, 1280, 1408, 1536, 1664, 2304, 2432]
                fm_isq = [1, 1, 0, 0, 1, 1, 1, 1, 0, 0, 0, 0, 1, 1]
                bank_rr = [0]
                blk_rr = [0]

                def nb():
                    i = bank_rr[0] % 6
                    bank_rr[0] += 1
                    return i

                def run(xT_src, ntok_total, projT, va_scr, xbf_scr, kb_out, vb_out, ka_out, va_out, keep):
                    nsb = (ntok_total + 511) // 512
                    xsrc3 = xT_src.rearrange("(k p) t -> p k t", p=128)
                    xbf3 = xbf_scr.rearrange("(k p) t -> p k t", p=128)
                    proj3 = projT.rearrange("(f p) t -> p f t", p=128)

                    def load(sb_i):
                        sl = sb_i % 2
                        c0 = sb_i * 512
                        n = min(512, ntok_total - c0)
                        sc.dma("sp", xf[sl][:, :, 0:n], xsrc3[:, :, c0:c0 + n], writes=[xf_b[sl]])
                        sc.op("pool", lambda e: e.tensor_copy(out=xb[sl][:, :, 0:n], in_=xf[sl][:, :, 0:n]),
                              reads=[xf_b[sl]], writes=[xb_b[sl]])

                    load(0)
                    for sb_i in range(nsb):
                        sl = sb_i % 2
                        c0 = sb_i * 512
                        n = min(512, ntok_total - c0)
                        if sb_i + 1 < nsb:
                            load(sb_i + 1)
                        sc.dma("sp", xbf3[:, :, c0:c0 + n], xb[sl][:, :, 0:n], reads=[xb_b[sl]])
                        for fi in range(14):
                            bi = nb()
                            for k in range(KD):
                                sc.op("pe", lambda e, k=k, fi=fi, bi=bi: e.matmul(
                                    bk(bi, n), lhsT=win[:, k, fm_cols[fi]:fm_cols[fi] + 128], rhs=xb[sl][:, k, 0:n],
                                    start=(k == 0), stop=(k == KD - 1)),
                                    reads=[win_b, xb_b[sl]], writes=[bank_b[bi]])
                            evac(fm[sl][:, fi, 0:n], bk(bi, n), [bank_b[bi]], [fm_b[sl]],
                                 scale=(0.125 if fm_isq[fi] else None))
                        sc.dma("sp", proj3[:, :, c0:c0 + n], fm[sl][:, :, 0:n], reads=[fm_b[sl]])
                        nblk = (n + 127) // 128
                        for tb in range(nblk):
                            m = min(128, n - tb * 128)
                            r0 = c0 + tb * 128
                            ts = blk_rr[0] % 2
                            blk_rr[0] += 1
                            is_keep = r0 < keep
                            for gi, (col0, dst) in enumerate(((256, "a"), (1280, "k"), (1792, "v"))):
                                bi = nb()
                                for k in range(KD):
                                    sc.op("pe", lambda e, k=k, bi=bi, col0=col0: e.matmul(
                                        bk(bi, 512, m), lhsT=xb[sl][:, k, tb * 128:tb * 128 + m],
                                        rhs=win[:, k, col0:col0 + 512], start=(k == 0), stop=(k == KD - 1)),
                                        reads=[win_b, xb_b[sl]], writes=[bank_b[bi]])
                                if dst == "a":
                                    evac(tav[ts][0:m, :], bk(bi, 512, m)[:, 256:512], [bank_b[bi]], [tav_b[ts]])
                                    if is_keep:
                                        evac(tka[ts][0:m, :], bk(bi, 512, m), [bank_b[bi]], [tka_b[ts]])
                                elif dst == "k":
                                    evac(tkv[ts][0:m, 0, :], bk(bi, 512, m), [bank_b[bi]], [tkv_b[ts]])
                                else:
                                    evac(tkv[ts][0:m, 1, :], bk(bi, 512, m), [bank_b[bi]], [tkv_b[ts]])
                            sc.dma("sp", va_scr[r0:r0 + m, :], tav[ts][0:m, :], reads=[tav_b[ts]])
                            sc.dma("sp", kb_out[r0:r0 + m, :], tkv[ts][0:m, 0, :], reads=[tkv_b[ts]])
                            sc.dma("sp", vb_out[r0:r0 + m, :], tkv[ts][0:m, 1, :], reads=[tkv_b[ts]])
                            if is_keep:
                                sc.dma("sp", ka_out[r0:r0 + m, :], tka[ts][0:m, 0:256], reads=[tka_b[ts]])
                                sc.dma("sp", va_out[r0:r0 + m, :], tka[ts][0:m, 256:512], reads=[tka_b[ts]])

                if DBG & 4:
                    sc.barrier()
                    return
                mf = sbt(ps, "mf", [128, KD, 256], F32)
                mb = sbt(ps, "mb", [128, KD, 256], BF16)
                mfb, mbb = Buf(), Buf()
                sc.dma("sp", mf[:], memT.rearrange("(k p) t -> p k t", p=128), writes=[mfb])
                sc.op("pool", lambda e: e.tensor_copy(out=mb[:], in_=mf[:]), reads=[mfb], writes=[mbb])
                mkt = sbt(ps, "mkt", [128, 2, 256], BF16)
                mktb = Buf()
                for fi in range(2):
                    bi = nb()
                    for k in range(KD):
                        sc.op("pe", lambda e, k=k, fi=fi, bi=bi: e.matmul(
                            bk(bi, 256), lhsT=wmem[:, k, fi * 128:(fi + 1) * 128], rhs=mb[:, k, :],
                            start=(k == 0), stop=(k == KD - 1)), reads=[wmem_b, mbb], writes=[bank_b[bi]])
                    evac(mkt[:, fi, :], bk(bi, 256), [bank_b[bi]], [mktb])
                sc.dma("sp", mkT_d.rearrange("(f p) t -> p f t", p=128), mkt[:], reads=[mktb])
                mtok = sbt(ps, "mtok", [128, 2, 512], F32)
                mvb = sbt(ps, "mvb", [128, 2, 256], BF16)
                mtokb, mvbb = Buf(), Buf()
                for tb in range(2):
                    bi = nb()
                    for k in range(KD):
                        sc.op("pe", lambda e, k=k, tb=tb, bi=bi: e.matmul(
                            bk(bi, 512), lhsT=mb[:, k, tb * 128:(tb + 1) * 128], rhs=wmem[:, k, :],
                            start=(k == 0), stop=(k == KD - 1)), reads=[wmem_b, mbb], writes=[bank_b[bi]])
                    evac(mtok[:, tb, :], bk(bi, 512), [bank_b[bi]], [mtokb])
                    evac(mvb[:, tb, :], bk(bi, 512)[:, 256:512], [bank_b[bi]], [mvbb])
                sc.dma("sp", pmkv_d.rearrange("(r p) f -> p r f", p=128), mtok[:], reads=[mtokb])
                sc.dma("sp", mv_d.rearrange("(r p) f -> p r f", p=128), mvb[:], reads=[mvbb])

                if DBG & 1:
                    run(xrT[:, 0:S], DBG_NTOK, projT_p, va_p, xbf_p, pbk_d, pbv_d, pak_d, pav_d, 512)
                if DBG & 2:
                    run(xsT, NS, projT_s, va_s, xbf_s, sbk_d, sbv_d, sak_d, sav_d, NS)
                sc.barrier()

        phase1()
        if STOP <= 1:
            sc.barrier()
            return nc


        def phase2():
            with ExitStack() as ps:
                kTf = sbt(ps, "kT", [128, 4 * S + 256], BF16)
                kT = kTf[:, 0:4 * S].rearrange("p (h s) -> p h s", h=4)
                kT_b = Buf()
                kTs = [kTf[:, i * 4 * NKS:(i + 1) * 4 * NKS].rearrange("p (h s) -> p h s", h=4) for i in range(2)]
                kTs_b = [Buf(), Buf()]
                dv = sbt(ps, "dv", [128, 66, 512], BF16)
                dv_b = Buf()
                dvs = [dv[:, 33 * i:33 * i + 33, :] for i in range(2)]
                dvs_b = [Buf(), Buf()]
                mkTs = [sbt(ps, "mkTs%d" % i, [128, 2, 256], BF16) for i in range(2)]
                mvs = [sbt(ps, "mvs%d" % i, [128, 2, 256], BF16) for i in range(2)]
                mks_b = [Buf(), Buf()]
                dst = [sbt(ps, "dst%d" % i, [128, 2, 1, 512], F32) for i in range(2)]
                dst_b = [Buf(), Buf()]
                qt = [sbt(ps, "qt%d" % i, [128, 16, 128], BF16) for i in range(2)]
                qt_b = [Buf(), Buf()]
                for i in range(2):
                    sc.op("pool", lambda e, i=i: e.memset(qt[i][:], 0.0), writes=[qt_b[i]])

                def load_q(sl_, src3, c0_, nq_):
                    for (h0, t0, nt) in ((0, 0, 2), (4, 4, 4), (12, 12, 2)):
                        qv = qt[sl_][:, h0:h0 + 2 * nt, :].rearrange("p (a b) q -> p a b q", b=2)
                        sc.dma("sp", qv[0:64, :, 0, 0:nq_], src3[0:64, t0:t0 + nt, c0_:c0_ + nq_], writes=[qt_b[sl_]])
                        sc.dma("sp", qv[64:128, :, 1, 0:nq_], src3[64:128, t0:t0 + nt, c0_:c0_ + nq_], writes=[qt_b[sl_]])
                vsh = [sbt(ps, "vsh%d" % i, [128, 512], F32) for i in range(2)]
                vsh_b = [Buf(), Buf()]
                beta = [sbt(ps, "beta%d" % i, [128, 512], F32) for i in range(2)]
                beta_b = [Buf(), Buf()]
                Pt = [sbt(ps, "Pt%d" % i, [128, 640], BF16) for i in range(2)]
                Pt_b = [Buf(), Buf()]
                pT = [sbt(ps, "pT%d" % i, [128, 640], BF16) for i in range(2)]
                pT_b = [Buf(), Buf()]
                osb = [sbt(ps, "osb%d" % i, [128, 512], BF16) for i in range(2)]
                osb_b = [Buf(), Buf()]
                oT = [sbt(ps, "oT%d" % i, [128, 4, 128], BF16) for i in range(2)]
                oT_b = [Buf(), Buf()]
                kaT = [sbt(ps, "kaT%d" % i, [128, 2, 640], BF16) for i in range(2)]
                kaT_b = [Buf(), Buf()]
                vaw = [sbt(ps, "vaw%d" % i, [128, 5, 256], BF16) for i in range(2)]
                vaw_b = [Buf(), Buf()]
                ssb = [sbt(ps, "ssb%d" % i, [128, 640], F32) for i in range(2)]
                ssb_b = [Buf(), Buf()]
                biasp = sbt(ps, "biasp", [128, 4, 640], F32)
                bias_b = Buf()
                mkT = sbt(ps, "mkT", [128, 2, 256], BF16)
                mv = sbt(ps, "mv", [128, 2, 256], BF16)
                mk_b = Buf()
                cstf = sbt(ps, "cst", [128, 1024], F32)
                c2 = cstf[:, :].rearrange("p (r f) -> p r f", f=512)
                c4 = cstf[:, :].rearrange("p (r f) -> p r f", f=256)
                cst_b = Buf()
                rr = {"z": 0, "t": 0, "b": 0, "p": 0, "pt": 0, "ss": 0}

                def nxt(k, n=2):
                    v = rr[k] % n
                    rr[k] += 1
                    return v

                sc.dma("sp", biasp[:], bias_p.rearrange("h q k -> q h k"), writes=[bias_b])
                sc.op("pool", lambda e: e.memset(biasp[0:64, :, 576:640], -1e30), reads=[bias_b], writes=[bias_b])
                sc.op("pool", lambda e: e.memset(biasp[64:128, :, 0:64], -1e30), reads=[bias_b], writes=[bias_b])

                def build_dv(pieces_fn, n, dvv, dvv_b):
                    nb_ = (n + 127) // 128
                    for g in range(nb_):
                        dv_block(pieces_fn, n, dvv, dvv_b, g)

                def dv_block(pieces_fn, n, dvv, dvv_b, g):
                    sl = nxt("b")
                    r0 = g * 128
                    m = min(128, n - r0)
                    for (d0, nr, src) in pieces_fn(r0, m):
                        sc.dma("sp", dst[sl][d0:d0 + nr, 0, 0, :], src, writes=[dst_b[sl]])
                    for (d0, nr, src) in pieces_fn(r0 + 1, m):
                        sc.dma("sp", dst[sl][d0:d0 + nr, 1, 0, :], src, writes=[dst_b[sl]])
                    sc.op("pool", lambda e: e.tensor_tensor(
                        out=dvv[0:m, g, :], in0=dst[sl][0:m, 1, 0, :], in1=dst[sl][0:m, 0, 0, :],
                        op=ALU.subtract), reads=[dst_b[sl]], writes=[dvv_b])

                def transposes(src_fn, nq, W, fdt=False):
                    ts = nxt("t")
                    ncb = (W + 127) // 128
                    for c in range(ncb):
                        wc = min(128, W - c * 128)
                        sc.op("pe", lambda e, c=c, wc=wc, ts=ts: e.transpose(
                            out=psT[ts][0:wc, c * 128:c * 128 + nq], in_=src_fn(c, wc), identity=ident[0:nq, 0:nq]),
                            reads=src_fn.bufs, writes=[psT_b[ts]])
                    return ts, ncb

                def evacT(ts, dst_tile, dst_buf, nq, ncb, diag):
                    src3 = psT[ts][:, 0:ncb * 128].rearrange("p (c q) -> p c q", q=128)
                    d3 = dst_tile[:, 0:ncb * 128].rearrange("p (c q) -> p c q", q=128)
                    sc.op("act", lambda e: e.copy(out=d3[:, 0:ncb, 0:nq], in_=src3[:, 0:ncb, 0:nq]),
                          reads=[psT_b[ts]], writes=[dst_buf, psT_b[ts]])
                    if diag:
                        sc.op("pool", lambda e: e.tensor_tensor(out=dst_tile[:, 0:nq], in0=dst_tile[:, 0:nq],
                                                                in1=lowmask[:, 0:nq], op=ALU.mult),
                              reads=[dst_buf, cbuf], writes=[dst_buf])

                NST = 6
                cry = sbt(ps, "cry", [128, 4], F32)
                cry_b = [Buf() for _ in range(4)]
                stt_ = [sbt(ps, "stat2_%d" % i, [128, 32], F32) for i in range(2)]
                stt_b = [Buf(), Buf()]
                oam = [sbt(ps, "oam%d" % i, [128, 512], BF16) for i in range(2)]
                oam_b = [Buf(), Buf()]

                def run_pipeline(tasks, hooks):
                    n = len(tasks)
                    for i in range(n + NST - 1):
                        if i in hooks:
                            for hfn in hooks[i]:
                                hfn()
                        for s_ in range(NST - 1, -1, -1):
                            t = i - s_
                            if 0 <= t < n:
                                tasks[t][s_]()

                def b_tasks(q_tile, q_buf, nq, k0, nkeys, vshift, vshift_b, oslot, oT_dst, c0, kTv, kTv_b, dvv, dvv_b):
                    tasks = []
                    per_head = [[] for _ in range(8)]
                    for h in range(8):
                        r0, hp = (h % 2) * 64, h // 2
                        obank = 4 + (h % 2)
                        C = {"prev": None}
                        ntile = (nkeys - k0 + 511) // 512
                        for ti in range(ntile):
                            start = k0 + ti * 512
                            W = min(512, nkeys - start)
                            ncb = (W + 127) // 128
                            T = {}

                            def s0(T=T, start=start, W=W, r0=r0, hp=hp, h=h):
                                zb = nxt("zb")
                                T["zb"] = zb
                                sc.op("pe", lambda e: e.matmul(
                                    bk(zb, W, nq), lhsT=q_tile[:, 4 + h, 0:nq], rhs=kTv[:, hp, start:start + W],
                                    start=True, stop=True), reads=[q_buf, kTv_b], writes=[bank_b[zb]])

                            def s1(T=T, W=W):
                                zb = T["zb"]
                                bs = nxt("p")
                                T["bs"] = bs
                                sc.op("act", lambda e: e.activation(
                                    out=beta[bs][0:nq, 0:W], in_=bk(zb, W, nq), func=AF.Sigmoid, scale=-1.0),
                                    reads=[bank_b[zb]], writes=[beta_b[bs], bank_b[zb]])

                            def s2(T=T, W=W, C=C, ti=ti, h=h, ntile=ntile):
                                bs = T["bs"]
                                pt = nxt("pt")
                                T["pt"] = pt
                                prev = C["prev"]
                                init = 1.0 if prev is None else cry[0:nq, prev:prev + 1]
                                d1 = trimask if ti == 0 else zeros
                                rd = [beta_b[bs], cbuf] + ([cry_b[prev]] if prev is not None else [])
                                sc.op("dve", lambda e: e.tensor_tensor_scan(
                                    out=Pt[pt][0:nq, 0:W], data0=beta[bs][0:nq, 0:W], data1=d1[0:nq, 0:W], initial=init,
                                    op0=ALU.mult, op1=ALU.max), reads=rd, writes=[Pt_b[pt]])
                                if ti < ntile - 1:
                                    cs = (h % 2) * 2 + (ti % 2)
                                    sc.op("pool", lambda e: e.tensor_copy(out=cry[0:nq, cs:cs + 1], in_=Pt[pt][0:nq, W - 1:W]),
                                          reads=[Pt_b[pt]], writes=[cry_b[cs]])
                                    C["prev"] = cs

                            def s3(T=T, W=W):
                                pt = T["pt"]

                                def src_fn(c, wc):
                                    return Pt[pt][0:nq, c * 128:c * 128 + wc]
                                src_fn.bufs = [Pt_b[pt], cbuf]
                                T["ts"], _ = transposes(src_fn, nq, W)

                            def s4(T=T, ncb=ncb, ti=ti):
                                pts = nxt("ss")
                                T["pts"] = pts
                                evacT(T["ts"], pT[pts], pT_b[pts], nq, ncb, ti == 0)

                            def s5(T=T, W=W, ncb=ncb, ti=ti, ntile=ntile, start=start, h=h):
                                pts = T["pts"]
                                obank = 4 + (h % 2)
                                for c in range(ncb):
                                    wc = min(128, W - c * 128)
                                    kb_ = start // 128 + c
                                    sc.op("pe", lambda e, c=c, wc=wc, kb_=kb_: e.matmul(
                                        bk(obank, 512, nq)[:, h * 64:(h + 1) * 64], lhsT=pT[pts][0:wc, c * 128:c * 128 + nq],
                                        rhs=dvv[0:wc, kb_, h * 64:(h + 1) * 64], start=(ti == 0 and c == 0),
                                        stop=(ti == ntile - 1 and c == ncb - 1)),
                                        reads=[pT_b[pts], dvv_b], writes=[bank_b[obank]])
                                if h == 7 and ti == ntile - 1:
                                    def par(ap, par_):
                                        return ap.rearrange("p (a b d) -> p a b d", b=2, d=64)[:, :, par_, :]
                                    for par_ in range(2):
                                        ob_ = 4 + par_
                                        sc.op("dve", lambda e, par_=par_, ob_=ob_: e.tensor_tensor(
                                            out=par(osb[oslot][0:nq, 0:512], par_), in0=par(bk(ob_, 512, nq), par_),
                                            in1=par(vshift[0:nq, :], par_), op=ALU.add),
                                            reads=[bank_b[ob_], vshift_b], writes=[osb_b[oslot], bank_b[ob_]])
                                    finish(osb[oslot], osb_b[oslot], nq, oT_dst, 2, c0)
                            per_head[h].append([s0, s1, s2, s3, s4, s5])
                    for p_ in range(4):
                        for ti in range(len(per_head[2 * p_])):
                            tasks.append(per_head[2 * p_][ti])
                            tasks.append(per_head[2 * p_ + 1][ti])
                    return tasks

                def sm_tasks(q_tile, q_buf, qbase, nq, kt_fn, kt_bufs, nk, v_fn, v_bufs, bias_t, col0, oslot, st, last,
                             oT_dst, c0):
                    tasks = []
                    obank = 5
                    ncb = (nk + 127) // 128
                    so = col0 // 64
                    for h in range(4):
                        r0, hp = (h % 2) * 64, h // 2
                        T = {}
                        pab = [bank_b[2], bank_b[3]]

                        def s0(T=T, r0=r0, hp=hp, h=h):
                            for c_ in range(0, nk, 512):
                                w = min(512, nk - c_)
                                sc.op("pe", lambda e, c_=c_, w=w: e.matmul(
                                    psB[0:nq, c_:c_ + w], lhsT=q_tile[:, qbase + h, 0:nq], rhs=kt_fn(r0, hp, c_, w),
                                    start=True, stop=True), reads=[q_buf] + kt_bufs, writes=pab)

                        def s1(T=T, h=h):
                            ss = nxt("ssb")
                            T["ss"] = ss
                            sc.op("act", lambda e: e.copy(out=ssb[ss][0:nq, 0:nk], in_=psB[0:nq, 0:nk]),
                                  reads=pab, writes=[ssb_b[ss]] + pab)
                            if bias_t is not None:
                                sc.op("pool", lambda e: e.tensor_tensor(
                                    out=ssb[ss][0:nq, 0:nk], in0=ssb[ss][0:nq, 0:nk], in1=bias_t[0:nq, h, 0:nk], op=ALU.add),
                                    reads=[ssb_b[ss], bias_b], writes=[ssb_b[ss]])
                            sc.op("dve", lambda e: e.reduce_max(out=stt_[st][0:nq, 16 + so + h:17 + so + h],
                                                                in_=ssb[ss][0:nq, 0:nk], axis=AX.X),
                                  reads=[ssb_b[ss]], writes=[stt_b[st]])
                            sc.op("dve", lambda e: e.tensor_scalar(
                                out=stt_[st][0:nq, 24 + so + h:25 + so + h], in0=stt_[st][0:nq, 16 + so + h:17 + so + h],
                                scalar1=-1.0, scalar2=None, op0=ALU.mult), reads=[stt_b[st]], writes=[stt_b[st]])

                        def s2(T=T, h=h):
                            ss = T["ss"]
                            pt = nxt("pt")
                            T["pt"] = pt
                            sc.op("act", lambda e: e.activation(
                                out=Pt[pt][0:nq, 0:nk], in_=ssb[ss][0:nq, 0:nk], func=AF.Exp,
                                bias=stt_[st][0:nq, 24 + so + h:25 + so + h], scale=1.0,
                                accum_out=stt_[st][0:nq, so + h:so + h + 1]), reads=[ssb_b[ss], stt_b[st]],
                                writes=[Pt_b[pt], stt_b[st]])

                        def s3(T=T):
                            pt = T["pt"]

                            def src_fn(c, wc):
                                return Pt[pt][0:nq, c * 128:c * 128 + wc]
                            src_fn.bufs = [Pt_b[pt], cbuf]
                            T["ts"], _ = transposes(src_fn, nq, nk)

                        def s4(T=T):
                            pts = nxt("ss")
                            T["pts"] = pts
                            evacT(T["ts"], pT[pts], pT_b[pts], nq, ncb, False)

                        def s5(T=T, h=h):
                            pts = T["pts"]
                            for c in range(ncb):
                                wc = min(128, nk - c * 128)
                                sc.op("pe", lambda e, c=c, wc=wc: e.matmul(
                                    bk(obank, 512, nq)[:, col0 + h * 64:col0 + (h + 1) * 64],
                                    lhsT=pT[pts][0:wc, c * 128:c * 128 + nq], rhs=v_fn(c, wc, h),
                                    start=(c == 0), stop=(c == ncb - 1)),
                                    reads=[pT_b[pts]] + v_bufs, writes=[bank_b[obank]])
                            if last and h == 3:
                                sc.op("dve", lambda e: e.reciprocal(out=stt_[st][0:nq, 8:16], in_=stt_[st][0:nq, 0:8]),
                                      reads=[stt_b[st]], writes=[stt_b[st]])
                                for hh in range(8):
                                    sc.op("dve", lambda e, hh=hh: e.tensor_scalar(
                                        out=oam[oslot][0:nq, hh * 64:(hh + 1) * 64],
                                        in0=bk(obank, 512, nq)[:, hh * 64:(hh + 1) * 64], scalar1=stt_[st][0:nq, 8 + hh:9 + hh],
                                        scalar2=None, op0=ALU.mult), reads=[bank_b[obank], stt_b[st]],
                                        writes=[oam_b[oslot], bank_b[obank]])
                                finish(oam[oslot], oam_b[oslot], nq, oT_dst, 0, c0)
                        tasks.append([s0, s1, s2, s3, s4, s5])
                    return tasks

                rr.update({"zb": 0, "ssb": 0, "ot": 0})

                def finish(src, src_b, nq, oT_dst, which, c0):
                    os_ = nxt("ot")

                    def src_fn(c, wc):
                        return src[0:nq, c * 128:c * 128 + 128]
                    src_fn.bufs = [src_b, cbuf]
                    ts, ncb = transposes(src_fn, nq, 512)
                    src3 = psT[ts][:, 0:512].rearrange("p (c q) -> p c q", q=128)
                    evac(oT[os_][:, 0:4, 0:nq], src3[:, :, 0:nq], [psT_b[ts]], [oT_b[os_]])
                    d3 = oT_dst.rearrange("(f p) t -> p f t", p=128)
                    if which == 2:
                        sc.dma("sp", d3[:, 2:6, c0:c0 + nq], oT[os_][:, 0:4, 0:nq], reads=[oT_b[os_]])
                    else:
                        sc.dma("sp", d3[:, 0:2, c0:c0 + nq], oT[os_][:, 0:2, 0:nq], reads=[oT_b[os_]])
                        sc.dma("sp", d3[:, 6:8, c0:c0 + nq], oT[os_][:, 2:4, 0:nq], reads=[oT_b[os_]])

                if DBG & 1:
                    proj3 = projT_p.rearrange("(f p) t -> p f t", p=128)
                    for t4 in range(4):
                        sc.dma("sp", kT[:, t4, :], projT_p[(8 + t4) * 128:(9 + t4) * 128, :], writes=[kT_b])
                    build_dv(lambda j0, m: [(0, m, pbv_d[j0:j0 + m, :])], S, dv, dv_b)
                    sc.dma("sp", mkT[:], mkT_d.rearrange("(f p) t -> p f t", p=128), writes=[mk_b])
                    sc.dma("sp", mv[:], mv_d.rearrange("(r p) f -> p r f", p=128), writes=[mk_b])
                    nblocks = DBG_NTOK // 128
                    tasks, hooks = [], {}

                    def loads(jb):
                        sl = jb % 2
                        c0 = jb * 128
                        nkA = min(640, S - c0)
                        load_q(sl, proj3, c0, 128)
                        sc.dma("sp", vsh[sl][:], pbv_d[c0 + 1:c0 + 129, :], writes=[vsh_b[sl]])
                        sc.dma("sp", kaT[sl][:, :, 0:nkA], proj3[:, 2:4, c0:c0 + nkA], writes=[kaT_b[sl]])
                        sc.dma("sp", vaw[sl][:, 0:nkA // 128, :],
                               va_p[c0:c0 + nkA, :].rearrange("(r p) f -> p r f", p=128), writes=[vaw_b[sl]])

                    for jb in range(nblocks):
                        sl = jb % 2
                        c0 = jb * 128
                        nkA = min(640, S - c0)
                        first = len(tasks)
                        if jb == 0:
                            hooks.setdefault(0, []).append(lambda: loads(0))
                            if nblocks > 1:
                                hooks.setdefault(0, []).append(lambda: loads(1))
                        elif jb + 1 < nblocks:
                            hooks.setdefault(first + NST, []).append(lambda jb=jb: loads(jb + 1))
                        tasks += sm_tasks(qt[sl], qt_b[sl], 0, 128,
                                          lambda r0, hp, c_, w, sl=sl: kaT[sl][:, hp, c_:c_ + w],
                                          [kaT_b[sl]], nkA, lambda c, wc, h, sl=sl: vaw[sl][0:wc, c, h * 64:(h + 1) * 64],
                                          [vaw_b[sl]], biasp, 0, sl, sl, False, oT_p, c0)
                        tasks += sm_tasks(qt[sl], qt_b[sl], 12, 128, lambda r0, hp, c_, w: mkT[:, hp, c_:c_ + w],
                                          [mk_b], 256, lambda c, wc, h: mv[0:wc, c, h * 64:(h + 1) * 64], [mk_b], None,
                                          256, sl, sl, True, oT_p, c0)
                        tasks += b_tasks(qt[sl], qt_b[sl], 128, c0, S, vsh[sl], vsh_b[sl], sl, oT_p, c0, kT, kT_b, dv, dv_b)
                    run_pipeline(tasks, hooks)

                if DBG & 2:
                    sc.barrier()
                    proj3s = projT_s.rearrange("(f p) t -> p f t", p=128)
                    biass = biasp
                    sc.dma("sp", biasp[0:16, :, 0:528], bias_s.rearrange("h q k -> q h k"), writes=[bias_b])

                    def prep(b):
                        sl = b % 2
                        c0 = b * 16

                        def vr_pieces(j0, m):
                            out = []
                            j, end = j0, j0 + m
                            while j < end:
                                if j < 16:
                                    e_ = min(end, 16)
                                    out.append((j - j0, e_ - j, sbv_d[c0 + j:c0 + e_, :]))
                                elif j < NKS:
                                    e_ = min(end, NKS)
                                    out.append((j - j0, e_ - j, cbv[b, j - 16:e_ - 16, :]))
                                else:
                                    e_ = end
                                    out.append((j - j0, 1, pbv_d[S:S + 1, :]))
                                j = e_
                            return out

                        load_q(sl, proj3s, c0, 16)
                        sc.dma("sp", kTs[sl][:, :, 0:16], proj3s[:, 8:12, c0:c0 + 16], writes=[kTs_b[sl]])
                        dvi = [0]
                        ndvb = (NKS + 127) // 128
                        for g in range(0, 32, 2):
                            sc.dma("sp", c2, cbk[b, g * 128:(g + 2) * 128, :].rearrange("(r p) f -> p r f", p=128),
                                   writes=[cst_b])
                            for hp in range(4):
                                zb = nxt("z", 4)
                                for i in range(2):
                                    sc.op("pe", lambda e, i=i, hp=hp, zb=zb: e.transpose(
                                        out=bk(zb)[:, i * 128:(i + 1) * 128], in_=c2[:, i, hp * 128:(hp + 1) * 128],
                                        identity=identf[:]), reads=[cst_b, cbuf], writes=[bank_b[zb]])
                                evac(kTs[sl][:, hp, 16 + g * 128:16 + (g + 2) * 128], bk(zb, 256), [bank_b[zb]], [kTs_b[sl]])
                            for _ in range(2):
                                if dvi[0] < ndvb:
                                    dv_block(vr_pieces, NKS, dvs[sl], dvs_b[sl], dvi[0])
                                    dvi[0] += 1
                        while dvi[0] < ndvb:
                            dv_block(vr_pieces, NKS, dvs[sl], dvs_b[sl], dvi[0])
                            dvi[0] += 1
                        for (d0, nr, src) in vr_pieces(1, 16):
                            sc.dma("sp", vsh[sl][d0:d0 + nr, :], src, writes=[vsh_b[sl]])
                        sc.dma("sp", kaT[sl][:, :, 512:528], proj3s[:, 2:4, c0:c0 + 16], writes=[kaT_b[sl]])
                        sc.dma("sp", c4, cak[b].rearrange("(r p) f -> p r f", p=128), writes=[cst_b])
                        for hp in range(2):
                            zb = nxt("z", 4)
                            for i in range(4):
                                sc.op("pe", lambda e, i=i, hp=hp, zb=zb: e.transpose(
                                    out=bk(zb)[:, i * 128:(i + 1) * 128], in_=c4[:, i, hp * 128:(hp + 1) * 128],
                                    identity=identf[:]), reads=[cst_b, cbuf], writes=[bank_b[zb]])
                            evac(kaT[sl][:, hp, 0:512], bk(zb), [bank_b[zb]], [kaT_b[sl]])
                        sc.dma("sp", c4, cav[b].rearrange("(r p) f -> p r f", p=128), writes=[cst_b])
                        sc.op("pool", lambda e, sl=sl: e.tensor_copy(out=vaw[sl][:, 0:4, :], in_=c4),
                              reads=[cst_b], writes=[vaw_b[sl]])
                        sc.dma("sp", vaw[sl][0:16, 4, :], va_s[c0:c0 + 16, :], writes=[vaw_b[sl]])
                        sc.dma("sp", c4[:, 0:2, :], cmk[b].rearrange("(r p) f -> p r f", p=128), writes=[cst_b])
                        for hp in range(2):
                            zb = nxt("z", 4)
                            for i in range(2):
                                sc.op("pe", lambda e, i=i, hp=hp, zb=zb: e.transpose(
                                    out=bk(zb)[:, i * 128:(i + 1) * 128], in_=c4[:, i, hp * 128:(hp + 1) * 128],
                                    identity=identf[:]), reads=[cst_b, cbuf], writes=[bank_b[zb]])
                            evac(mkTs[sl][:, hp, :], bk(zb, 256), [bank_b[zb]], [mks_b[sl]])
                        sc.dma("sp", c4[:, 2:4, :], cmv[b].rearrange("(r p) f -> p r f", p=128), writes=[cst_b])
                        sc.op("pool", lambda e, sl=sl: e.tensor_copy(out=mvs[sl][:], in_=c4[:, 2:4, :]),
                              reads=[cst_b], writes=[mks_b[sl]])

                    def run_tasks(b):
                        sl = b % 2
                        c0 = b * 16
                        tasks = sm_tasks(qt[sl], qt_b[sl], 0, 16, lambda r0, hp, c_, w: kaT[sl][:, hp, c_:c_ + w],
                                         [kaT_b[sl]], 528, lambda c, wc, h: vaw[sl][0:wc, c, h * 64:(h + 1) * 64],
                                         [vaw_b[sl]], biass, 0, sl, sl, False, oT_s, c0)
                        tasks += sm_tasks(qt[sl], qt_b[sl], 12, 16, lambda r0, hp, c_, w: mkTs[sl][:, hp, c_:c_ + w],
                                          [mks_b[sl]], 256, lambda c, wc, h: mvs[sl][0:wc, c, h * 64:(h + 1) * 64],
                                          [mks_b[sl]], None, 256, sl, sl, True, oT_s, c0)
                        tasks += b_tasks(qt[sl], qt_b[sl], 16, 0, NKS, vsh[sl], vsh_b[sl], sl, oT_s, c0,
                                         kTs[sl], kTs_b[sl], dvs[sl], dvs_b[sl])
                        run_pipeline(tasks, {})

                    prep(0)
                    prep(1)
                    run_tasks(0)
                    prep(2)
                    run_tasks(1)
                    prep(3)
                    run_tasks(2)
                    run_tasks(3)
                sc.barrier()

        phase2()
        if STOP <= 2:
            sc.barrier()
            return nc

        def layer_norm(ps_tiles, r, r_b, m, g_t, b_t, out, out_b):
            st6, mvt, st_b = ps_tiles
            for hf in range(2):
                sc.op("dve", lambda e, hf=hf: e.bn_stats(out=st6[0:m, hf, :], in_=r[0:m, hf * 512:(hf + 1) * 512]),
                      reads=[r_b], writes=[st_b])
            sc.op("dve", lambda e: e.bn_aggr(out=mvt[0:m, 0:2], in_=st6[0:m, :, :].rearrange("p a b -> p (a b)")),
                  reads=[st_b], writes=[st_b])
            sc.op("dve", lambda e: e.tensor_scalar(out=mvt[0:m, 2:3], in0=mvt[0:m, 1:2], scalar1=EPS, scalar2=None,
                                                   op0=ALU.add), reads=[st_b], writes=[st_b])
            sc.op("act", lambda e: e.activation(out=mvt[0:m, 3:4], in_=mvt[0:m, 2:3], func=AF.Sqrt),
                  reads=[st_b], writes=[st_b])
            sc.op("dve", lambda e: e.reciprocal(out=mvt[0:m, 4:5], in_=mvt[0:m, 3:4]), reads=[st_b], writes=[st_b])
            sc.op("dve", lambda e: e.tensor_scalar(out=out[0:m, :], in0=r[0:m, :], scalar1=mvt[0:m, 0:1],
                                                   scalar2=mvt[0:m, 4:5], op0=ALU.subtract, op1=ALU.mult),
                  reads=[r_b, st_b], writes=[out_b])
            sc.op("pool", lambda e: e.tensor_tensor(out=out[0:m, :], in0=out[0:m, :], in1=g_t[0:m, :], op=ALU.mult),
                  reads=[out_b, cbuf], writes=[out_b])
            sc.op("pool", lambda e: e.tensor_tensor(out=out[0:m, :], in0=out[0:m, :], in1=b_t[0:m, :], op=ALU.add),
                  reads=[out_b, cbuf], writes=[out_b])

        def phase3a():
            with ExitStack() as ps:
                stage = [sbt(ps, "wst%d" % i, [128, 1024], F32) for i in range(3)]
                stage_b = [Buf(), Buf(), Buf()]
                wg, wg_b = load_weight_bf(ps, "wg", w_gate, D, 3 * D, stage, stage_b)
                wp = sbt(ps, "wp", [128, 8, D], BF16)
                wp_b = Buf()
                i = 0
                for (wap, nk, k0) in ((w_pa, 2, 0), (w_pb, 4, 2), (w_pm, 2, 6)):
                    for k in range(nk):
                        st, stb = stage[i % 2], stage_b[i % 2]
                        sc.dma("sp", st[:], wap[k * 128:(k + 1) * 128, :], writes=[stb])
                        sc.op("pool", lambda e, st=st, kk=k0 + k: e.tensor_copy(out=wp[:, kk, :], in_=st[:]),
                              reads=[stb], writes=[wp_b])
                        i += 1
                wo, wo_b = load_weight_bf(ps, "wo", w_o, D, D, stage, stage_b)
                bg = sbt(ps, "bg", [128, 24], F32)
                lg = sbt(ps, "lg", [128, D], F32)
                lb = sbt(ps, "lb", [128, D], F32)
                sc.dma("sp", bg[:], b_gateT, writes=[cbuf])
                sc.dma("sp", lg[:], ln1_g, writes=[cbuf])
                sc.dma("sp", lb[:], ln1_b, writes=[cbuf])
                xb = [sbt(ps, "xb%d" % i, [128, KD, 512], BF16) for i in range(2)]
                xb_b = [Buf(), Buf()]
                ot = [sbt(ps, "ot%d" % i, [128, 8, 512], BF16) for i in range(2)]
                ot_b = [Buf(), Buf()]
                gs = [sbt(ps, "gs%d" % i, [128, 512], F32) for i in range(2)]
                gs_b = [Buf(), Buf()]
                tmp2 = [[sbt(ps, "tmp%d_%d" % (i, j), [128, 512], F32) for i in range(3)] for j in range(2)]
                tmp2_b = [[Buf(), Buf(), Buf()] for j in range(2)]
                hT = sbt(ps, "hT", [128, 8, 512], BF16)
                hT_b = Buf()
                xt = [sbt(ps, "xt%d" % i, [128, D], F32) for i in range(2)]
                xt_b = [Buf(), Buf()]
                r1 = [sbt(ps, "r1%d" % i, [128, D], F32) for i in range(2)]
                r1_b = [Buf(), Buf()]
                h1 = [sbt(ps, "h1%d" % i, [128, D], F32) for i in range(2)]
                h1_b = [Buf(), Buf()]
                h1bf = sbt(ps, "h1bf", [128, D], BF16)
                h1bf_b = Buf()
                h1t = [sbt(ps, "h1t%d" % i, [128, 8, 128], BF16) for i in range(2)]
                h1t_b = [Buf(), Buf()]
                st6 = sbt(ps, "st6", [128, 2, 6], F32)
                mvt = sbt(ps, "mvt", [128, 8], F32)
                lnt = (st6, mvt, Buf())
                rr = {"z": 0, "t": 0, "blk": 0}

                def nb():
                    v = rr["z"] % 6
                    rr["z"] += 1
                    return v

                h1bf2 = sbt(ps, "h1bf2", [128, D], BF16)
                h1bfs = [h1bf, h1bf2]
                h1bfs_b = [h1bf_b, Buf()]
                deferred = []

                def flush():
                    while deferred:
                        deferred.pop(0)()

                def run(ntok, xbf_scr, oT_scr, x_tm, h1_scr, h1T_scr):
                    xbf3 = xbf_scr.rearrange("(k p) t -> p k t", p=128)
                    oT3 = oT_scr.rearrange("(k p) t -> p k t", p=128)
                    h1T3 = h1T_scr.rearrange("(k p) t -> p k t", p=128)
                    ng = (ntok + 511) // 512

                    def gload(g):
                        sl = g % 2
                        c0 = g * 512
                        n = min(512, ntok - c0)
                        sc.dma("sp", xb[sl][:, :, 0:n], xbf3[:, :, c0:c0 + n], writes=[xb_b[sl]])
                        sc.dma("sp", ot[sl][:, :, 0:n], oT3[:, :, c0:c0 + n], writes=[ot_b[sl]])

                    def xload(r0, m, ts):
                        sc.dma("sp", xt[ts][0:m, :], x_tm[r0:r0 + m, :], writes=[xt_b[ts]])

                    gload(0)
                    for g in range(ng):
                        sl = g % 2
                        c0 = g * 512
                        n = min(512, ntok - c0)
                        if g + 1 < ng:
                            gload(g + 1)
                        nblk = (n + 127) // 128
                        ts0 = rr["blk"]
                        xload(c0, min(128, n), ts0 % 2)
                        for ft in range(8):
                            tmp, tmp_b = tmp2[ft % 2], tmp2_b[ft % 2]
                            for br, (k0, nk) in enumerate(((0, 2), (2, 4), (6, 2))):
                                bi = nb()
                                for k in range(KD):
                                    sc.op("pe", lambda e, k=k, bi=bi, br=br: e.matmul(
                                        bk(bi, n), lhsT=wg[:, k, br * D + ft * 128:br * D + ft * 128 + 128],
                                        rhs=xb[sl][:, k, 0:n], start=(k == 0), stop=(k == KD - 1)),
                                        reads=[wg_b, xb_b[sl]], writes=[bank_b[bi]])
                                gsl = rr["t"] % 2
                                rr["t"] += 1
                                sc.op("act", lambda e, bi=bi, gsl=gsl, br=br: e.activation(
                                    out=gs[gsl][:, 0:n], in_=bk(bi, n), func=AF.Sigmoid,
                                    bias=bg[:, br * 8 + ft:br * 8 + ft + 1], scale=1.0),
                                    reads=[bank_b[bi], cbuf], writes=[gs_b[gsl], bank_b[bi]])
                                b2 = nb()
                                for kk in range(nk):
                                    sc.op("pe", lambda e, kk=kk, b2=b2, k0=k0, nk=nk: e.matmul(
                                        bk(b2, n), lhsT=wp[:, k0 + kk, ft * 128:(ft + 1) * 128], rhs=ot[sl][:, k0 + kk, 0:n],
                                        start=(kk == 0), stop=(kk == nk - 1)),
                                        reads=[wp_b, ot_b[sl]], writes=[bank_b[b2]])
                                sc.op("dve", lambda e, b2=b2, gsl=gsl, br=br: e.tensor_tensor(
                                    out=tmp[br][:, 0:n], in0=bk(b2, n), in1=gs[gsl][:, 0:n], op=ALU.mult),
                                    reads=[bank_b[b2], gs_b[gsl]], writes=[tmp_b[br], bank_b[b2]])
                            sc.op("pool", lambda e: e.tensor_tensor(out=tmp[0][:, 0:n], in0=tmp[0][:, 0:n], in1=tmp[1][:, 0:n],
                                                                    op=ALU.add), reads=[tmp_b[0], tmp_b[1]], writes=[tmp_b[0]])
                            sc.op("pool", lambda e, ft=ft: e.tensor_tensor(out=hT[:, ft, 0:n], in0=tmp[0][:, 0:n],
                                                                           in1=tmp[2][:, 0:n], op=ALU.add),
                                  reads=[tmp_b[0], tmp_b[2]], writes=[hT_b])
                            if ft == 0:
                                flush()
                        for tb in range(nblk):
                            m = min(128, n - tb * 128)
                            r0 = c0 + tb * 128
                            ts = rr["blk"] % 2
                            rr["blk"] += 1
                            banks2 = []
                            for hf in range(2):
                                bi = nb()
                                banks2.append(bi)
                                for ft in range(8):
                                    sc.op("pe", lambda e, ft=ft, bi=bi, hf=hf: e.matmul(
                                        bk(bi, 512, m), lhsT=hT[:, ft, tb * 128:tb * 128 + m],
                                        rhs=wo[:, ft, hf * 512:(hf + 1) * 512], start=(ft == 0), stop=(ft == 7)),
                                        reads=[wo_b, hT_b], writes=[bank_b[bi]])
                            flush()
                            if tb + 1 < nblk:
                                xload(r0 + 128, min(128, n - (tb + 1) * 128), (ts + 1) % 2)
                            for hf in range(2):
                                bi = banks2[hf]
                                sc.op("dve", lambda e, bi=bi, hf=hf: e.scalar_tensor_tensor(
                                    out=r1[ts][0:m, hf * 512:(hf + 1) * 512], in0=xt[ts][0:m, hf * 512:(hf + 1) * 512],
                                    scalar=ALPHA, in1=bk(bi, 512, m), op0=ALU.mult, op1=ALU.add),
                                    reads=[xt_b[ts], bank_b[bi]], writes=[r1_b[ts], bank_b[bi]])
                            layer_norm(lnt, r1[ts], r1_b[ts], m, lg, lb, h1[ts], h1_b[ts])
                            sc.dma("sp", h1_scr[r0:r0 + m, :], h1[ts][0:m, :], reads=[h1_b[ts]])
                            sc.op("act", lambda e, ts=ts, m=m: e.copy(out=h1bfs[ts][0:m, :], in_=h1[ts][0:m, :]),
                                  reads=[h1_b[ts]], writes=[h1bfs_b[ts]])

                            def tail(ts=ts, m=m, r0=r0):
                                for hf in range(2):
                                    tsl = rr["t"] % 2
                                    rr["t"] += 1
                                    for c in range(4):
                                        sc.op("pe", lambda e, c=c, hf=hf, tsl=tsl: e.transpose(
                                            out=psT[tsl][:, c * 128:c * 128 + m],
                                            in_=h1bfs[ts][0:m, hf * 512 + c * 128:hf * 512 + (c + 1) * 128],
                                            identity=ident[0:m, 0:m]), reads=[h1bfs_b[ts], cbuf], writes=[psT_b[tsl]])
                                    src3 = psT[tsl][:, 0:512].rearrange("p (c q) -> p c q", q=128)
                                    evac(h1t[ts][:, hf * 4:hf * 4 + 4, 0:m], src3[:, :, 0:m], [psT_b[tsl]], [h1t_b[ts]])
                                sc.dma("sp", h1T3[:, :, r0:r0 + m], h1t[ts][:, :, 0:m], reads=[h1t_b[ts]])
                            deferred.append(tail)
                    flush()

                if DBG & 1:
                    run(DBG_NTOK, xbf_p, oT_p, xr, h1_p, h1T_p)
                if DBG & 2:
                    run(NS, xbf_s, oT_s, xs, h1_s, h1T_s)
                sc.barrier()

        phase3a()
        if STOP <= 3:
            sc.barrier()
            return nc

        def phase3b():
            with ExitStack() as ps:
                stage = [sbt(ps, "wst%d" % i, [128, 1024], F32) for i in range(2)]
                stage_b = [Buf(), Buf()]
                wup = sbt(ps, "wup", [128, KD, 2 * DFF], BF16)
                wup_pb = [Buf() for _ in range(6)]
                wdn = sbt(ps, "wdn", [128, NFT, D], BF16)
                wdn_b = Buf()
                ld_i = [0]

                def load_chunk(w_ap, r0, c0, c1, dst_ap, dst_buf):
                    i = ld_i[0]
                    ld_i[0] += 1
                    st, stb = stage[i % len(stage)], stage_b[i % len(stage)]
                    sc.dma("sp", st[:, 0:c1 - c0], w_ap[r0:r0 + 128, c0:c1], writes=[stb])
                    ce = ("pool", "dve", "act")[i % 3]
                    if ce == "act":
                        sc.op(ce, lambda e: e.copy(out=dst_ap, in_=st[:, 0:c1 - c0]), reads=[stb], writes=[dst_buf])
                    else:
                        sc.op(ce, lambda e: e.tensor_copy(out=dst_ap, in_=st[:, 0:c1 - c0]), reads=[stb], writes=[dst_buf])

                def load_up_piece(pc):
                    c0 = pc * 1024
                    c1 = min(2 * DFF, c0 + 1024)
                    for k in range(KD):
                        load_chunk(w_up, k * 128, c0, c1, wup[:, k, c0:c1], wup_pb[pc])

                def load_dn(k0, k1):
                    for k in range(k0, k1):
                        load_chunk(w_down, k * 128, 0, D, wdn[:, k, :], wdn_b)

                load_up_piece(0)
                load_up_piece(2)
                pending = [lambda: load_up_piece(3), lambda: load_up_piece(1), lambda: load_up_piece(4),
                           lambda: load_up_piece(5), lambda: load_dn(0, 6), lambda: load_dn(6, 12),
                           lambda: load_dn(12, 18), lambda: load_dn(18, NFT)]
                cw = sbt(ps, "cw", [128, 2 * NFT, 3], F32)
                cb = sbt(ps, "cb", [128, 2 * NFT], F32)
                lg = sbt(ps, "lg", [128, D], F32)
                lb = sbt(ps, "lb", [128, D], F32)
                stt = sbt(ps, "stt", [128, 2 * NFT, 4, 2], F32)
                sco = sbt(ps, "sco", [128, 2 * NFT, 4, 2], F32)
                carry = sbt(ps, "carry", [128, 2 * NFT, 2], F32)
                carry_b, sco_b = Buf(), Buf()
                sc.dma("sp", cw[:], conv_wT, writes=[cbuf])
                sc.dma("sp", cb[:], conv_bT, writes=[cbuf])
                sc.dma("sp", lg[:], ln2_g, writes=[cbuf])
                sc.dma("sp", lb[:], ln2_b, writes=[cbuf])
                sc.dma("sp", stt[:], stT, writes=[cbuf])
                sc.op("pool", lambda e: e.memset(carry[:], 0.0), writes=[carry_b])
                GW = 256
                hT = [sbt(ps, "hT%d" % i, [128, KD, GW], BF16) for i in range(2)]
                hT_b = [Buf(), Buf()]
                NEX = 7
                ext = [sbt(ps, "ext%d" % i, [128, GW + 8], F32) for i in range(NEX)]
                ext_b = [Buf() for _ in range(NEX)]
                exth_b = [Buf() for _ in range(NEX)]
                cc = [sbt(ps, "cc%d" % i, [128, GW], F32) for i in range(NEX)]
                cc_b = [Buf() for _ in range(NEX)]
                aT = sbt(ps, "aT", [128, NFT, GW], BF16)
                aT_b = Buf()
                ht = [sbt(ps, "ht%d" % i, [128, D], F32) for i in range(2)]
                ht_b = [Buf(), Buf()]
                r2 = [sbt(ps, "r2%d" % i, [128, D], F32) for i in range(2)]
                r2_b = [Buf(), Buf()]
                st6 = sbt(ps, "st6", [128, 2, 6], F32)
                mvt = sbt(ps, "mvt", [128, 8], F32)
                lnt = (st6, mvt, Buf())
                rr = {"z": 0, "e": 0, "blk": 0, "g": 0}

                def nb():
                    v = rr["z"] % 6
                    rr["z"] += 1
                    return v

                def hload(h1T3, c0, n, slot):
                    sc.dma("sp", hT[slot][:, :, 0:n], h1T3[:, :, c0:c0 + n], writes=[hT_b[slot]])

                def group(h1T3, c0, n, nseg, L, h1_scr, y_out, sample, pre=None):
                    sl = rr["g"] % 2
                    rr["g"] += 1
                    if pre is not None:
                        pre((sl + 1) % 2)
                    nblk_ = (n + 127) // 128
                    for tb in range(nblk_):
                        m_ = min(128, n - tb * 128)
                        sc.dma("sp", ht[(rr["blk"] + tb) % 2][0:m_, :], h1_scr[c0 + tb * 128:c0 + tb * 128 + m_, :],
                               writes=[ht_b[(rr["blk"] + tb) % 2]])
                    def stA(fft):
                        info = []
                        for part in range(2):
                            idx = part * NFT + fft
                            col = idx * 128
                            bi = nb()
                            for k in range(KD):
                                sc.op("pe", lambda e, k=k, bi=bi, col=col: e.matmul(
                                    bk(bi, n), lhsT=wup[:, k, col:col + 128], rhs=hT[sl][:, k, 0:n],
                                    start=(k == 0), stop=(k == KD - 1)), reads=[wup_pb[col // 1024], hT_b[sl]],
                                    writes=[bank_b[bi]])
                            es_ = rr["e"] % NEX
                            rr["e"] += 1
                            e3 = ext[es_][:, 0:nseg * (L + 2)].rearrange("p (s l) -> p s l", l=L + 2)
                            if sample:
                                sc.op("pool", lambda e, e3=e3, idx=idx: e.tensor_copy(out=e3[:, :, L:L + 2], in_=stt[:, idx, :, :]),
                                      reads=[cbuf], writes=[exth_b[es_]])
                            else:
                                sc.op("pool", lambda e, e3=e3, idx=idx: e.tensor_copy(out=e3[:, 0, L:L + 2], in_=carry[:, idx, :]),
                                      reads=[carry_b], writes=[exth_b[es_]])
                            sc.op("act", lambda e, bi=bi, e3=e3: e.copy(
                                out=e3[:, :, 0:L], in_=bk(bi, n).rearrange("p (s l) -> p s l", l=L)),
                                reads=[bank_b[bi]], writes=[ext_b[es_], bank_b[bi]])
                            if sample:
                                sc.op("pool", lambda e, e3=e3, idx=idx: e.tensor_copy(out=sco[:, idx, :, :], in_=e3[:, :, 0:2]),
                                      reads=[ext_b[es_]], writes=[sco_b])
                            else:
                                sc.op("pool", lambda e, e3=e3, idx=idx: e.tensor_copy(out=carry[:, idx, :], in_=e3[:, 0, 0:2]),
                                      reads=[ext_b[es_]], writes=[carry_b])
                            c3 = cc[es_][:, 0:n].rearrange("p (s l) -> p s l", l=L)
                            sc.op("act", lambda e, bi=bi, c3=c3, idx=idx: e.activation(
                                out=c3, in_=bk(bi, n).rearrange("p (s l) -> p s l", l=L), func=AF.Identity,
                                bias=cb[:, idx:idx + 1], scale=cw[:, idx, 2:3]),
                                reads=[bank_b[bi], cbuf], writes=[cc_b[es_], bank_b[bi]])
                            info.append((es_, e3, c3, idx))
                        return info

                    def stB(info):
                        for j in (1, 2):
                            for (es_, e3, c3, idx) in info:
                                sc.op("dve", lambda e, e3=e3, c3=c3, idx=idx, j=j: e.scalar_tensor_tensor(
                                    out=c3, in0=e3[:, :, j:L + j], scalar=cw[:, idx, 2 - j:3 - j], in1=c3,
                                    op0=ALU.mult, op1=ALU.add), reads=[ext_b[es_], exth_b[es_], cbuf, cc_b[es_]],
                                    writes=[cc_b[es_]])

                    def stC(info, fft):
                        g_, v_ = info[0][0], info[1][0]
                        sc.op("act", lambda e: e.activation(out=cc[g_][:, 0:n], in_=cc[g_][:, 0:n], func=AF.Silu),
                              reads=[cc_b[g_]], writes=[cc_b[g_]])
                        sc.op("pool", lambda e: e.tensor_tensor(
                            out=aT[:, fft, 0:n], in0=cc[g_][:, 0:n], in1=cc[v_][:, 0:n], op=ALU.mult),
                            reads=[cc_b[g_], cc_b[v_]], writes=[aT_b])

                    infos = {}
                    for it in range(NFT + 2):
                        if pending:
                            pending.pop(0)()
                        if it < NFT:
                            infos[it] = stA(it)
                        if 0 <= it - 1 < NFT:
                            stB(infos[it - 1])
                        if 0 <= it - 2 < NFT:
                            stC(infos[it - 2], it - 2)
                    for tb in range((n + 127) // 128):
                        m = min(128, n - tb * 128)
                        r0 = c0 + tb * 128
                        ts = rr["blk"] % 2
                        rr["blk"] += 1
                        for hf in range(2):
                            bi = nb()
                            for fft in range(NFT):
                                sc.op("pe", lambda e, fft=fft, bi=bi, hf=hf: e.matmul(
                                    bk(bi, 512, m), lhsT=aT[:, fft, tb * 128:tb * 128 + m],
                                    rhs=wdn[:, fft, hf * 512:(hf + 1) * 512], start=(fft == 0), stop=(fft == NFT - 1)),
                                    reads=[wdn_b, aT_b], writes=[bank_b[bi]])
                            sc.op("dve", lambda e, bi=bi, hf=hf: e.scalar_tensor_tensor(
                                out=r2[ts][0:m, hf * 512:(hf + 1) * 512], in0=ht[ts][0:m, hf * 512:(hf + 1) * 512],
                                scalar=ALPHA, in1=bk(bi, 512, m), op0=ALU.mult, op1=ALU.add),
                                reads=[ht_b[ts], bank_b[bi]], writes=[r2_b[ts], bank_b[bi]])
                        layer_norm(lnt, r2[ts], r2_b[ts], m, lg, lb, ht[ts], ht_b[ts])
                        sc.dma("sp", y_out[r0:r0 + m, :], ht[ts][0:m, :], reads=[ht_b[ts]])

                if DBG & 1:
                    h1T3 = h1T_p.rearrange("(k p) t -> p k t", p=128)
                    ngp = DBG_NTOK // GW
                    hload(h1T3, (ngp - 1) * GW, GW, rr["g"] % 2)
                    for g in range(ngp - 1, -1, -1):
                        pre = (lambda slot, g=g: hload(h1T3, (g - 1) * GW, GW, slot)) if g > 0 else None
                        group(h1T3, g * GW, GW, 1, GW, h1_p, yp_d, False, pre)
                    sc.dma("sp", pconv_d, carry[:], reads=[carry_b])
                if DBG & 2:
                    h1T3 = h1T_s.rearrange("(k p) t -> p k t", p=128)
                    hload(h1T3, 0, NS, rr["g"] % 2)
                    group(h1T3, 0, NS, 4, 16, h1_s, ys_d, True)
                    sc.dma("sp", sconv_d, sco[:], reads=[sco_b])
                sc.barrier()

        phase3b()

        sc.barrier()
    return nc


def _prep_inputs(inp, c):
    f = np.float32
    x = np.asarray(inp["x_prompt"][c], dtype=f)
    xr_ = np.ascontiguousarray(x[::-1])
    xrT_ = np.zeros((D, S + 1), dtype=f)
    xrT_[:, :S] = xr_.T
    bsl = slice(4 * c, 4 * c + 4)
    xs_ = np.asarray(inp["x_sample"][bsl], dtype=f)[:, ::-1, :].reshape(NS, D)
    m = {}
    m["xrT"] = xrT_
    m["xr"] = xr_
    m["memT"] = np.ascontiguousarray(np.asarray(inp["mem_prompt"][c], dtype=f).T)
    m["xsT"] = np.ascontiguousarray(xs_.T)
    m["xs"] = np.ascontiguousarray(xs_)
    m["cak"] = np.ascontiguousarray(np.asarray(inp["cache_a_k"][0, bsl], dtype=f)[:, ::-1].reshape(4, 512, 256))
    m["cav"] = np.ascontiguousarray(np.asarray(inp["cache_a_v"][0, bsl], dtype=f)[:, ::-1].reshape(4, 512, 256))
    m["cbk"] = np.ascontiguousarray(np.asarray(inp["cache_b_k"][0, bsl], dtype=f)[:, ::-1].reshape(4, PAST, 512))
    m["cbv"] = np.ascontiguousarray(np.asarray(inp["cache_b_v"][0, bsl], dtype=f)[:, ::-1].reshape(4, PAST, 512))
    m["cmk"] = np.ascontiguousarray(np.asarray(inp["cache_mem_k"][0, bsl], dtype=f).reshape(4, 256, 256))
    m["cmv"] = np.ascontiguousarray(np.asarray(inp["cache_mem_v"][0, bsl], dtype=f).reshape(4, 256, 256))
    st = np.asarray(inp["state_ffn_conv"][0, bsl], dtype=f)
    m["stT"] = np.ascontiguousarray(st[:, ::-1, :].reshape(4, 2, 2 * NFT, 128).transpose(3, 2, 0, 1))
    return m


def _shared_inputs(inp):
    f = np.float32
    m = {}
    tab = np.asarray(inp["rel_bias"][0], dtype=f)
    jq = np.arange(128)[:, None]
    jk = np.arange(640)[None, :]
    m["bias_p"] = np.ascontiguousarray(tab[:, np.clip(jk - jq, -256, 256) + 256])
    jq = np.arange(16)[:, None]
    jk = np.arange(528)[None, :]
    rel_new = (15 - jq) - (15 - (jk - 512))
    rel_cache = (15 - jq) + 1 + jk
    rel = np.where(jk >= 512, rel_new, rel_cache)
    m["bias_s"] = np.ascontiguousarray(tab[:, np.clip(rel, -256, 256) + 256])
    m["w_in"] = np.ascontiguousarray(np.asarray(inp["w_in"][0], dtype=f))
    m["w_mem"] = np.ascontiguousarray(np.asarray(inp["w_mem_kv"][0], dtype=f))
    m["w_pa"] = np.ascontiguousarray(np.asarray(inp["w_pa"][0], dtype=f))
    m["w_pb"] = np.ascontiguousarray(np.asarray(inp["w_pb"][0], dtype=f))
    m["w_pm"] = np.ascontiguousarray(np.asarray(inp["w_pm"][0], dtype=f))
    m["w_gate"] = np.ascontiguousarray(np.asarray(inp["w_gate"][0], dtype=f))
    m["b_gateT"] = np.ascontiguousarray(np.asarray(inp["b_gate"][0], dtype=f).reshape(24, 128).T)
    m["w_o"] = np.ascontiguousarray(np.asarray(inp["w_o"][0], dtype=f))
    m["ln1_g"] = np.ascontiguousarray(np.broadcast_to(np.asarray(inp["ln1_g"], dtype=f).reshape(1, D), (128, D)))
    m["ln1_b"] = np.ascontiguousarray(np.broadcast_to(np.asarray(inp["ln1_b"], dtype=f).reshape(1, D), (128, D)))
    m["w_up"] = np.ascontiguousarray(np.asarray(inp["w_up"][0], dtype=f))
    m["conv_wT"] = np.ascontiguousarray(np.asarray(inp["conv_w"][0], dtype=f).reshape(3, 2 * NFT, 128).transpose(2, 1, 0))
    m["conv_bT"] = np.ascontiguousarray(np.asarray(inp["conv_b"][0], dtype=f).reshape(2 * NFT, 128).T)
    m["w_down"] = np.ascontiguousarray(np.asarray(inp["w_down"][0], dtype=f))
    m["ln2_g"] = np.ascontiguousarray(np.broadcast_to(np.asarray(inp["ln2_g"], dtype=f).reshape(1, D), (128, D)))
    m["ln2_b"] = np.ascontiguousarray(np.broadcast_to(np.asarray(inp["ln2_b"], dtype=f).reshape(1, D), (128, D)))
    return m


def kernel(**inp):
    nc = build()
    shared = _shared_inputs(inp)
    in_maps = []
    for c in range(8):
        m = dict(shared)
        m.update(_prep_inputs(inp, c))
        in_maps.append(m)
    res = run_bass_kernel_spmd(nc, in_maps, core_ids=list(range(8)))
    R = res.results
    f = np.float32
    yp = np.stack([np.asarray(R[c]["yp_d"], dtype=f)[::-1] for c in range(8)])
    ys = np.concatenate([np.asarray(R[c]["ys_d"], dtype=f).reshape(4, 16, D)[:, ::-1] for c in range(8)])
    pak = np.stack([np.asarray(R[c]["pak_d"], dtype=f)[::-1].reshape(512, 4, 64) for c in range(8)])[None]
    pav = np.stack([np.asarray(R[c]["pav_d"], dtype=f)[::-1].reshape(512, 4, 64) for c in range(8)])[None]
    pbk = np.stack([np.asarray(R[c]["pbk_d"], dtype=f)[::-1].reshape(S, 8, 64) for c in range(8)])[None]
    pbv = np.stack([np.asarray(R[c]["pbv_d"], dtype=f)[:S][::-1].reshape(S, 8, 64) for c in range(8)])[None]
    pmk = np.stack([np.asarray(R[c]["pmkv_d"], dtype=f)[:, 0:256].reshape(256, 4, 64) for c in range(8)])[None]
    pmv = np.stack([np.asarray(R[c]["pmkv_d"], dtype=f)[:, 256:512].reshape(256, 4, 64) for c in range(8)])[None]
    pconv = np.stack([np.asarray(R[c]["pconv_d"], dtype=f).transpose(2, 1, 0).reshape(2, 2 * DFF)[::-1]
                      for c in range(8)])[None]
    sak = np.concatenate([np.asarray(R[c]["sak_d"], dtype=f).reshape(4, 16, 4, 64)[:, ::-1] for c in range(8)])[None]
    sav = np.concatenate([np.asarray(R[c]["sav_d"], dtype=f).reshape(4, 16, 4, 64)[:, ::-1] for c in range(8)])[None]
    sbk = np.concatenate([np.asarray(R[c]["sbk_d"], dtype=f).reshape(4, 16, 8, 64)[:, ::-1] for c in range(8)])[None]
    sbv = np.concatenate([np.asarray(R[c]["sbv_d"], dtype=f).reshape(4, 16, 8, 64)[:, ::-1] for c in range(8)])[None]
    sconv = np.concatenate([np.asarray(R[c]["sconv_d"], dtype=f).transpose(2, 3, 1, 0).reshape(4, 2, 2 * DFF)[:, ::-1]
                            for c in range(8)])[None]
    outs = (yp, ys, pak, pav, pbk, pbv, pmk, pmv, pconv, sak, sav, sbk, sbv, sconv)
    return tuple(np.ascontiguousarray(o, dtype=f) for o in outs)
```

```python
import os
import numpy as np
from contextlib import ExitStack
import concourse.bass as bass
import concourse.mybir as mybir
from concourse.bass_utils import run_bass_kernel_spmd

F32 = mybir.dt.float32
BF16 = mybir.dt.bfloat16
AF = mybir.ActivationFunctionType
ALU = mybir.AluOpType
AX = mybir.AxisListType

S = 8192
D = 1024
KD = 8
NBLK = 64
DFF = 2816
NFT = 22
NS = 64
PAST = 4096
NKS = PAST + 16
ALPHA = 2.0 ** 0.25
EPS = 1e-5
NDMA = 40
STOP = int(os.environ.get("MK_STOP", "99"))
LIMIT = int(os.environ.get("MK_LIMIT", "1000000000"))
DBG = int(os.environ.get("MK_DBG", "3"))
DBG_NTOK = int(os.environ.get("MK_NTOK", str(S)))


class Buf:
    __slots__ = ("w", "r")

    def __init__(self):
        self.w = None
        self.r = {}


class Sched:
    def __init__(self, nc, es):
        self.nc = nc
        self.eng = {"pe": nc.tensor, "act": nc.scalar, "dve": nc.vector, "pool": nc.gpsimd, "sp": nc.sync}
        self.semobj = {}
        self.cnt = {}
        for e in ("pe", "act", "dve", "pool"):
            self.semobj[e] = es.enter_context(nc.semaphore("s_" + e))
            self.cnt[e] = 0
        self.dma_tgt = [0] * NDMA
        for k in range(NDMA):
            self.semobj[("d", k)] = es.enter_context(nc.semaphore("s_d%d" % k))
        self.rr = 0
        self.seen = {e: {} for e in self.eng}

    def _need(self, reads, writes):
        need = {}
        for b in reads:
            if b.w is not None:
                k, v = b.w
                if need.get(k, 0) < v:
                    need[k] = v
        for b in writes:
            if b.w is not None:
                k, v = b.w
                if need.get(k, 0) < v:
                    need[k] = v
            for k, v in b.r.items():
                if need.get(k, 0) < v:
                    need[k] = v
        return need

    def _waits(self, e, need):
        seen = self.seen[e]
        for k, v in need.items():
            if k == e and e == "pe":
                continue
            if seen.get(k, 0) < v:
                self.eng[e].wait_ge(self.semobj[k], v)
                seen[k] = v

    def _mark(self, tok, reads, writes):
        k, v = tok
        for b in reads:
            if b.r.get(k, 0) < v:
                b.r[k] = v
        for b in writes:
            b.w = tok
            b.r = {}

    def op(self, e, fn, reads=(), writes=()):
        self.nops = getattr(self, "nops", 0) + 1
        if self.nops > LIMIT:
            return
        if self.nops == LIMIT:
            print("LAST OP", e, reads, writes)
        self._waits(e, self._need(reads, writes))
        inst = fn(self.eng[e])
        self.cnt[e] += 1
        inst.then_inc(self.semobj[e], 1)
        self._mark((e, self.cnt[e]), reads, writes)

    def dma(self, q, out, in_, reads=(), writes=()):
        self.nops = getattr(self, "nops", 0) + 1
        if self.nops > LIMIT:
            return
        if self.nops == LIMIT:
            print("LAST DMA", q, out, in_)
        need = self._need(reads, writes)
        k = self.rr
        self.rr = (k + 1) % NDMA
        key = ("d", k)
        if self.dma_tgt[k] > 0 and need.get(key, 0) < self.dma_tgt[k]:
            need[key] = self.dma_tgt[k]
        self._waits(q, need)
        inst = self.eng[q].dma_start(out=out, in_=in_)
        self.dma_tgt[k] += 16
        inst.then_inc(self.semobj[key], 16)
        self._mark((key, self.dma_tgt[k]), reads, writes)

    def barrier(self):
        for e in self.eng:
            need = {}
            for o in ("pe", "act", "dve", "pool"):
                if o != e and self.cnt[o] > 0:
                    need[o] = self.cnt[o]
            for k in range(NDMA):
                if self.dma_tgt[k] > 0:
                    need[("d", k)] = self.dma_tgt[k]
            seen = self.seen[e]
            for k, v in need.items():
                if seen.get(k, 0) < v:
                    self.eng[e].wait_ge(self.semobj[k], v)
                    seen[k] = v


def build():
    nc = bass.Bass("TRN2", target_bir_lowering=False)

    def din(name, shape, dt=F32):
        return nc.dram_tensor(name, list(shape), dt, kind="ExternalInput").ap()

    def dout(name, shape, dt=F32):
        return nc.dram_tensor(name, list(shape), dt, kind="ExternalOutput").ap()

    def dscr(name, shape, dt=BF16):
        return nc.dram_tensor(name, list(shape), dt, kind="Internal").ap()

    xrT = din("xrT", [D, S + 1])
    xr = din("xr", [S, D])
    memT = din("memT", [D, 256])
    xsT = din("xsT", [D, NS])
    xs = din("xs", [NS, D])
    cak = din("cak", [4, 512, 256])
    cav = din("cav", [4, 512, 256])
    cbk = din("cbk", [4, PAST, 512])
    cbv = din("cbv", [4, PAST, 512])
    cmk = din("cmk", [4, 256, 256])
    cmv = din("cmv", [4, 256, 256])
    stT = din("stT", [128, 2 * NFT, 4, 2])
    bias_p = din("bias_p", [4, 128, 640])
    bias_s = din("bias_s", [4, 16, 528])
    w_in = din("w_in", [D, 2560])
    w_mem = din("w_mem", [D, 512])
    w_pa = din("w_pa", [256, D])
    w_pb = din("w_pb", [512, D])
    w_pm = din("w_pm", [256, D])
    w_gate = din("w_gate", [D, 3 * D])
    b_gateT = din("b_gateT", [128, 24])
    w_o = din("w_o", [D, D])
    ln1_g = din("ln1_g", [128, D])
    ln1_b = din("ln1_b", [128, D])
    w_up = din("w_up", [D, 2 * DFF])
    conv_wT = din("conv_wT", [128, 2 * NFT, 3])
    conv_bT = din("conv_bT", [128, 2 * NFT])
    w_down = din("w_down", [DFF, D])
    ln2_g = din("ln2_g", [128, D])
    ln2_b = din("ln2_b", [128, D])

    yp_d = dout("yp_d", [S, D])
    ys_d = dout("ys_d", [NS, D])
    pak_d = dout("pak_d", [512, 256])
    pav_d = dout("pav_d", [512, 256])
    pbk_d = dout("pbk_d", [S, 512])
    pbv_d = dout("pbv_d", [S + 1, 512])
    pmkv_d = dout("pmkv_d", [256, 512])
    pconv_d = dout("pconv_d", [128, 2 * NFT, 2])
    sak_d = dout("sak_d", [NS, 256])
    sav_d = dout("sav_d", [NS, 256])
    sbk_d = dout("sbk_d", [NS, 512])
    sbv_d = dout("sbv_d", [NS, 512])
    sconv_d = dout("sconv_d", [128, 2 * NFT, 4, 2])

    projT_p = dscr("projT_p", [1792, S])
    projT_s = dscr("projT_s", [1792, NS])
    va_p = dscr("va_p", [S, 256])
    va_s = dscr("va_s", [NS, 256])
    xbf_p = dscr("xbf_p", [D, S])
    xbf_s = dscr("xbf_s", [D, NS])
    mkT_d = dscr("mkT_d", [256, 256])
    mv_d = dscr("mv_d", [256, 256])
    oT_p = dscr("oT_p", [1024, S])
    oT_s = dscr("oT_s", [1024, NS])
    h1_p = dscr("h1_p", [S, D], F32)
    h1_s = dscr("h1_s", [NS, D], F32)
    h1T_p = dscr("h1T_p", [D, S + 2])
    h1T_s = dscr("h1T_s", [D, NS])

    es = ExitStack()
    with es:
        sc = Sched(nc, es)

        uniq = [0]

        def sbt(stack, name, shape, dt):
            uniq[0] += 1
            return stack.enter_context(nc.sbuf_tensor("%s_%d" % (name, uniq[0]), list(shape), dt))

        psA = es.enter_context(nc.psum_tensor("psA", [128, 1024], F32))
        psB = es.enter_context(nc.psum_tensor("psB", [128, 1024], F32))
        psC = es.enter_context(nc.psum_tensor("psC", [128, 512], F32))
        psD = es.enter_context(nc.psum_tensor("psD", [128, 512], F32))
        psT0 = es.enter_context(nc.psum_tensor("psT0", [128, 1024], BF16))
        psT1 = es.enter_context(nc.psum_tensor("psT1", [128, 1024], BF16))
        banks = [(psA, 0), (psA, 512), (psB, 0), (psB, 512), (psC, 0), (psD, 0)]
        bank_b = [Buf() for _ in range(6)]
        psT = [psT0, psT1]
        psT_b = [Buf(), Buf()]

        def bk(i, n=512, parts=128):
            t, o = banks[i]
            return t[0:parts, o:o + n]

        ident = sbt(es, "ident", [128, 128], BF16)
        identf = sbt(es, "identf", [128, 128], F32)
        zeros = sbt(es, "zeros", [128, 512], F32)
        trimask = sbt(es, "trimask", [128, 512], F32)
        lowmask = sbt(es, "lowmask", [128, 128], F32)
        cbuf = Buf()
        sc.op("pool", lambda e: e.memset(zeros[:], 0.0), writes=[cbuf])
        sc.op("pool", lambda e: e.memset(identf[:], 1.0), writes=[cbuf])
        sc.op("pool", lambda e: e.affine_select(out=identf[:], in_=identf[:], pattern=[[-1, 128]], compare_op=ALU.is_equal,
                                                fill=0.0, base=0, channel_multiplier=1), reads=[cbuf], writes=[cbuf])
        sc.op("pool", lambda e: e.tensor_copy(out=ident[:], in_=identf[:]), reads=[cbuf], writes=[cbuf])
        sc.op("pool", lambda e: e.memset(trimask[:], 0.0), writes=[cbuf])
        sc.op("pool", lambda e: e.memset(trimask[:, 0:128], 1.0), reads=[cbuf], writes=[cbuf])
        sc.op("pool", lambda e: e.affine_select(out=trimask[:, 0:128], in_=trimask[:, 0:128], pattern=[[-1, 128]],
                                                compare_op=ALU.is_ge, fill=0.0, base=0, channel_multiplier=1),
              reads=[cbuf], writes=[cbuf])
        sc.op("pool", lambda e: e.memset(lowmask[:], 1.0), writes=[cbuf])
        sc.op("pool", lambda e: e.affine_select(out=lowmask[:], in_=lowmask[:], pattern=[[-1, 128]],
                                                compare_op=ALU.is_gt, fill=0.0, base=0, channel_multiplier=1),
              reads=[cbuf], writes=[cbuf])
        sc.barrier()

        evac_rr = [0]

        def evac(out, in_, reads, writes, scale=None, eng=None):
            writes = list(writes) + list(reads)
            if eng is None:
                eng = "act" if evac_rr[0] % 2 == 0 else "dve"
                evac_rr[0] += 1
            if eng == "act":
                if scale is None:
                    sc.op("act", lambda e: e.copy(out=out, in_=in_), reads, writes)
                else:
                    sc.op("act", lambda e: e.mul(out, in_, float(scale)), reads, writes)
            else:
                if scale is None:
                    sc.op("dve", lambda e: e.tensor_copy(out=out, in_=in_), reads, writes)
                else:
                    sc.op("dve", lambda e: e.tensor_scalar(out=out, in0=in_, scalar1=float(scale), scalar2=None,
                                                           op0=ALU.mult), reads, writes)

        def load_weight_bf(stack, name, w_ap, rows, cols, stage, stage_b):
            nk = rows // 128
            wt = sbt(stack, name, [128, nk, cols], BF16)
            wb = Buf()
            cw = stage[0].shape[1]
            i = 0
            for k in range(nk):
                for c0 in range(0, cols, cw):
                    c1 = min(cols, c0 + cw)
                    st, stb = stage[i % len(stage)], stage_b[i % len(stage)]
                    sc.dma("sp", st[:, 0:c1 - c0], w_ap[k * 128:(k + 1) * 128, c0:c1], writes=[stb])
                    ce = ("pool", "dve", "act")[i % 3]
                    if ce == "act":
                        sc.op(ce, lambda e, st=st, k=k, c0=c0, c1=c1: e.copy(out=wt[:, k, c0:c1], in_=st[:, 0:c1 - c0]),
                              reads=[stb], writes=[wb])
                    else:
                        sc.op(ce, lambda e, st=st, k=k, c0=c0, c1=c1: e.tensor_copy(out=wt[:, k, c0:c1], in_=st[:, 0:c1 - c0]),
                              reads=[stb], writes=[wb])
                    i += 1
            return wt, wb

        def phase1():
            with ExitStack() as ps:
                stage = [sbt(ps, "wst%d" % i, [128, 2560], F32) for i in range(2)]
                stage_b = [Buf(), Buf()]
                win, win_b = load_weight_bf(ps, "win", w_in, D, 2560, stage, stage_b)
                wmem, wmem_b = load_weight_bf(ps, "wmem", w_mem, D, 512, stage, stage_b)
                xf = [sbt(ps, "xf%d" % i, [128, KD, 512], F32) for i in range(2)]
                xf_b = [Buf(), Buf()]
                xb = [sbt(ps, "xb%d" % i, [128, KD, 512], BF16) for i in range(2)]
                xb_b = [Buf(), Buf()]
                fm = [sbt(ps, "fm%d" % i, [128, 14, 512], BF16) for i in range(2)]
                fm_b = [Buf(), Buf()]
                tkv = [sbt(ps, "tkv%d" % i, [128, 2, 512], F32) for i in range(2)]
                tkv_b = [Buf(), Buf()]
                tav = [sbt(ps, "tav%d" % i, [128, 256], BF16) for i in range(2)]
                tav_b = [Buf(), Buf()]
                tka = [sbt(ps, "tka%d" % i, [128, 512], F32) for i in range(2)]
                tka_b = [Buf(), Buf()]
                zrow = sbt(ps, "zrow", [1, 512], F32)
                zb = Buf()
                sc.op("pool", lambda e: e.memset(zrow[:], 0.0), writes=[zb])
                sc.dma("sp", pbv_d[S:S + 1, :], zrow[:], reads=[zb])

                fm_cols = [0, 128, 256, 384, 768, 896, 1024, 1152, 1280, 1408, 1536, 1664, 2304, 2432]
                fm_isq = [1, 1, 0, 0, 1, 1, 1, 1, 0, 0, 0, 0, 1, 1]
                bank_rr = [0]
                blk_rr = [0]

                def nb():
                    i = bank_rr[0] % 6
                    bank_rr[0] += 1
                    return i

                def run(xT_src, ntok_total, projT, va_scr, xbf_scr, kb_out, vb_out, ka_out, va_out, keep):
                    nsb = (ntok_total + 511) // 512
                    xsrc3 = xT_src.rearrange("(k p) t -> p k t", p=128)
                    xbf3 = xbf_scr.rearrange("(k p) t -> p k t", p=128)
                    proj3 = projT.rearrange("(f p) t -> p f t", p=128)

                    def load(sb_i):
                        sl = sb_i % 2
                        c0 = sb_i * 512
                        n = min(512, ntok_total - c0)
                        sc.dma("sp", xf[sl][:, :, 0:n], xsrc3[:, :, c0:c0 + n], writes=[xf_b[sl]])
                        sc.op("pool", lambda e: e.tensor_copy(out=xb[sl][:, :, 0:n], in_=xf[sl][:, :, 0:n]),
                              reads=[xf_b[sl]], writes=[xb_b[sl]])

                    load(0)
                    for sb_i in range(nsb):
                        sl = sb_i % 2
                        c0 = sb_i * 512
                        n = min(512, ntok_total - c0)
                        if sb_i + 1 < nsb:
                            load(sb_i + 1)
                        sc.dma("sp", xbf3[:, :, c0:c0 + n], xb[sl][:, :, 0:n], reads=[xb_b[sl]])
                        for fi in range(14):
                            bi = nb()
                            for k in range(KD):
                                sc.op("pe", lambda e, k=k, fi=fi, bi=bi: e.matmul(
                                    bk(bi, n), lhsT=win[:, k, fm_cols[fi]:fm_cols[fi] + 128], rhs=xb[sl][:, k, 0:n],
                                    start=(k == 0), stop=(k == KD - 1)),
                                    reads=[win_b, xb_b[sl]], writes=[bank_b[bi]])
                            evac(fm[sl][:, fi, 0:n], bk(bi, n), [bank_b[bi]], [fm_b[sl]],
                                 scale=(0.125 if fm_isq[fi] else None))
                        sc.dma("sp", proj3[:, :, c0:c0 + n], fm[sl][:, :, 0:n], reads=[fm_b[sl]])
                        nblk = (n + 127) // 128
                        for tb in range(nblk):
                            m = min(128, n - tb * 128)
                            r0 = c0 + tb * 128
                            ts = blk_rr[0] % 2
                            blk_rr[0] += 1
                            is_keep = r0 < keep
                            for gi, (col0, dst) in enumerate(((256, "a"), (1280, "k"), (1792, "v"))):
                                bi = nb()
                                c_lo, wN = (512, 256) if (dst == "a" and not is_keep) else (col0, 512)
                                for k in range(KD):
                                    sc.op("pe", lambda e, k=k, bi=bi, c_lo=c_lo, wN=wN: e.matmul(
                                        bk(bi, wN, m), lhsT=xb[sl][:, k, tb * 128:tb * 128 + m],
                                        rhs=win[:, k, c_lo:c_lo + wN], start=(k == 0), stop=(k == KD - 1)),
                                        reads=[win_b, xb_b[sl]], writes=[bank_b[bi]])
                                if dst == "a":
                                    evac(tav[ts][0:m, :], bk(bi, wN, m)[:, wN - 256:wN], [bank_b[bi]], [tav_b[ts]])
                                    if is_keep:
                                        evac(tka[ts][0:m, :], bk(bi, 512, m), [bank_b[bi]], [tka_b[ts]])
                                elif dst == "k":
                                    evac(tkv[ts][0:m, 0, :], bk(bi, 512, m), [bank_b[bi]], [tkv_b[ts]])
                                else:
                                    evac(tkv[ts][0:m, 1, :], bk(bi, 512, m), [bank_b[bi]], [tkv_b[ts]])
                            sc.dma("sp", va_scr[r0:r0 + m, :], tav[ts][0:m, :], reads=[tav_b[ts]])
                            sc.dma("sp", kb_out[r0:r0 + m, :], tkv[ts][0:m, 0, :], reads=[tkv_b[ts]])
                            sc.dma("sp", vb_out[r0:r0 + m, :], tkv[ts][0:m, 1, :], reads=[tkv_b[ts]])
                            if is_keep:
                                sc.dma("sp", ka_out[r0:r0 + m, :], tka[ts][0:m, 0:256], reads=[tka_b[ts]])
                                sc.dma("sp", va_out[r0:r0 + m, :], tka[ts][0:m, 256:512], reads=[tka_b[ts]])

                if DBG & 4:
                    sc.barrier()
                    return
                mf = sbt(ps, "mf", [128, KD, 256], F32)
                mb = sbt(ps, "mb", [128, KD, 256], BF16)
                mfb, mbb = Buf(), Buf()
                sc.dma("sp", mf[:], memT.rearrange("(k p) t -> p k t", p=128), writes=[mfb])
                sc.op("pool", lambda e: e.tensor_copy(out=mb[:], in_=mf[:]), reads=[mfb], writes=[mbb])
                mkt = sbt(ps, "mkt", [128, 2, 256], BF16)
                mktb = Buf()
                for fi in range(2):
                    bi = nb()
                    for k in range(KD):
                        sc.op("pe", lambda e, k=k, fi=fi, bi=bi: e.matmul(
                            bk(bi, 256), lhsT=wmem[:, k, fi * 128:(fi + 1) * 128], rhs=mb[:, k, :],
                            start=(k == 0), stop=(k == KD - 1)), reads=[wmem_b, mbb], writes=[bank_b[bi]])
                    evac(mkt[:, fi, :], bk(bi, 256), [bank_b[bi]], [mktb])
                sc.dma("sp", mkT_d.rearrange("(f p) t -> p f t", p=128), mkt[:], reads=[mktb])
                mtok = sbt(ps, "mtok", [128, 2, 512], F32)
                mvb = sbt(ps, "mvb", [128, 2, 256], BF16)
                mtokb, mvbb = Buf(), Buf()
                for tb in range(2):
                    bi = nb()
                    for k in range(KD):
                        sc.op("pe", lambda e, k=k, tb=tb, bi=bi: e.matmul(
                            bk(bi, 512), lhsT=mb[:, k, tb * 128:(tb + 1) * 128], rhs=wmem[:, k, :],
                            start=(k == 0), stop=(k == KD - 1)), reads=[wmem_b, mbb], writes=[bank_b[bi]])
                    evac(mtok[:, tb, :], bk(bi, 512), [bank_b[bi]], [mtokb])
                    evac(mvb[:, tb, :], bk(bi, 512)[:, 256:512], [bank_b[bi]], [mvbb])
                sc.dma("sp", pmkv_d.rearrange("(r p) f -> p r f", p=128), mtok[:], reads=[mtokb])
                sc.dma("sp", mv_d.rearrange("(r p) f -> p r f", p=128), mvb[:], reads=[mvbb])

                if DBG & 1:
                    run(xrT[:, 0:S], DBG_NTOK, projT_p, va_p, xbf_p, pbk_d, pbv_d, pak_d, pav_d, 512)
                if DBG & 2:
                    run(xsT, NS, projT_s, va_s, xbf_s, sbk_d, sbv_d, sak_d, sav_d, NS)
                sc.barrier()

        phase1()
        if STOP <= 1:
            sc.barrier()
            return nc


        def phase2():
            with ExitStack() as ps:
                kTf = sbt(ps, "kT", [128, 4 * S + 256], BF16)
                kT = kTf[:, 0:4 * S].rearrange("p (h s) -> p h s", h=4)
                kT_b = Buf()
                kTs = [kTf[:, i * 4 * NKS:(i + 1) * 4 * NKS].rearrange("p (h s) -> p h s", h=4) for i in range(2)]
                kTs_b = [Buf(), Buf()]
                dv = sbt(ps, "dv", [128, 66, 512], BF16)
                dv_b = Buf()
                dvs = [dv[:, 33 * i:33 * i + 33, :] for i in range(2)]
                dvs_b = [Buf(), Buf()]
                mkTs = [sbt(ps, "mkTs%d" % i, [128, 2, 256], BF16) for i in range(2)]
                mvs = [sbt(ps, "mvs%d" % i, [128, 2, 256], BF16) for i in range(2)]
                mks_b = [Buf(), Buf()]
                dst = [sbt(ps, "dst%d" % i, [128, 2, 1, 512], F32) for i in range(2)]
                dst_b = [Buf(), Buf()]
                qt = [sbt(ps, "qt%d" % i, [128, 16, 128], BF16) for i in range(2)]
                qt_b = [Buf(), Buf()]
                for i in range(2):
                    sc.op("pool", lambda e, i=i: e.memset(qt[i][:], 0.0), writes=[qt_b[i]])

                def load_q(sl_, src3, c0_, nq_):
                    for (h0, t0, nt) in ((0, 0, 2), (4, 4, 4), (12, 12, 2)):
                        qv = qt[sl_][:, h0:h0 + 2 * nt, :].rearrange("p (a b) q -> p a b q", b=2)
                        sc.dma("sp", qv[0:64, :, 0, 0:nq_], src3[0:64, t0:t0 + nt, c0_:c0_ + nq_], writes=[qt_b[sl_]])
                        sc.dma("sp", qv[64:128, :, 1, 0:nq_], src3[64:128, t0:t0 + nt, c0_:c0_ + nq_], writes=[qt_b[sl_]])
                vsh = [sbt(ps, "vsh%d" % i, [128, 512], F32) for i in range(2)]
                vsh_b = [Buf(), Buf()]
                beta = [sbt(ps, "beta%d" % i, [128, 512], F32) for i in range(2)]
                beta_b = [Buf(), Buf()]
                Pt = [sbt(ps, "Pt%d" % i, [128, 640], BF16) for i in range(2)]
                Pt_b = [Buf(), Buf()]
                pT = [sbt(ps, "pT%d" % i, [128, 640], BF16) for i in range(2)]
                pT_b = [Buf(), Buf()]
                osb = [sbt(ps, "osb%d" % i, [128, 512], BF16) for i in range(2)]
                osb_b = [Buf(), Buf()]
                oT = [sbt(ps, "oT%d" % i, [128, 4, 128], BF16) for i in range(2)]
                oT_b = [Buf(), Buf()]
                kaT = [sbt(ps, "kaT%d" % i, [128, 2, 640], BF16) for i in range(2)]
                kaT_b = [Buf(), Buf()]
                vaw = [sbt(ps, "vaw%d" % i, [128, 5, 256], BF16) for i in range(2)]
                vaw_b = [Buf(), Buf()]
                ssb = [sbt(ps, "ssb%d" % i, [128, 640], F32) for i in range(2)]
                ssb_b = [Buf(), Buf()]
                biasp = sbt(ps, "biasp", [128, 4, 640], F32)
                bias_b = Buf()
                mkT = sbt(ps, "mkT", [128, 2, 256], BF16)
                mv = sbt(ps, "mv", [128, 2, 256], BF16)
                mk_b = Buf()
                cstf = sbt(ps, "cst", [128, 1024], F32)
                c2 = cstf[:, :].rearrange("p (r f) -> p r f", f=512)
                c4 = cstf[:, :].rearrange("p (r f) -> p r f", f=256)
                cst_b = Buf()
                rr = {"z": 0, "t": 0, "b": 0, "p": 0, "pt": 0, "ss": 0}

                def nxt(k, n=2):
                    v = rr[k] % n
                    rr[k] += 1
                    return v

                sc.dma("sp", biasp[:], bias_p.rearrange("h q k -> q h k"), writes=[bias_b])
                sc.op("pool", lambda e: e.memset(biasp[0:64, :, 576:640], -1e30), reads=[bias_b], writes=[bias_b])
                sc.op("pool", lambda e: e.memset(biasp[64:128, :, 0:64], -1e30), reads=[bias_b], writes=[bias_b])

                def build_dv(pieces_fn, n, dvv, dvv_b):
                    nb_ = (n + 127) // 128
                    for g in range(nb_):
                        dv_block(pieces_fn, n, dvv, dvv_b, g)

                def dv_block(pieces_fn, n, dvv, dvv_b, g):
                    sl = nxt("b")
                    r0 = g * 128
                    m = min(128, n - r0)
                    for (d0, nr, src) in pieces_fn(r0, m):
                        sc.dma("sp", dst[sl][d0:d0 + nr, 0, 0, :], src, writes=[dst_b[sl]])
                    for (d0, nr, src) in pieces_fn(r0 + 1, m):
                        sc.dma("sp", dst[sl][d0:d0 + nr, 1, 0, :], src, writes=[dst_b[sl]])
                    sc.op("pool", lambda e: e.tensor_tensor(
                        out=dvv[0:m, g, :], in0=dst[sl][0:m, 1, 0, :], in1=dst[sl][0:m, 0, 0, :],
                        op=ALU.subtract), reads=[dst_b[sl]], writes=[dvv_b])

                def transposes(src_fn, nq, W, fdt=False):
                    ts = nxt("t")
                    ncb = (W + 127) // 128
                    for c in range(ncb):
                        wc = min(128, W - c * 128)
                        sc.op("pe", lambda e, c=c, wc=wc, ts=ts: e.transpose(
                            out=psT[ts][0:wc, c * 128:c * 128 + nq], in_=src_fn(c, wc), identity=ident[0:nq, 0:nq]),
                            reads=src_fn.bufs, writes=[psT_b[ts]])
                    return ts, ncb

                def evacT(ts, dst_tile, dst_buf, nq, ncb, diag):
                    src3 = psT[ts][:, 0:ncb * 128].rearrange("p (c q) -> p c q", q=128)
                    d3 = dst_tile[:, 0:ncb * 128].rearrange("p (c q) -> p c q", q=128)
                    if diag:
                        sc.op("dve", lambda e: e.tensor_tensor(out=dst_tile[:, 0:nq], in0=psT[ts][:, 0:nq],
                                                               in1=lowmask[:, 0:nq], op=ALU.mult),
                              reads=[psT_b[ts]], writes=[dst_buf, psT_b[ts]])
                        if ncb > 1:
                            sc.op("act", lambda e: e.copy(out=d3[:, 1:ncb, 0:nq], in_=src3[:, 1:ncb, 0:nq]),
                                  reads=[psT_b[ts]], writes=[dst_buf, psT_b[ts]])
                    else:
                        sc.op("act", lambda e: e.copy(out=d3[:, 0:ncb, 0:nq], in_=src3[:, 0:ncb, 0:nq]),
                              reads=[psT_b[ts]], writes=[dst_buf, psT_b[ts]])

                NST = 6
                cry = sbt(ps, "cry", [128, 4], F32)
                cry_b = [Buf() for _ in range(4)]
                stt_ = [sbt(ps, "stat2_%d" % i, [128, 32], F32) for i in range(2)]
                stt_b = [Buf(), Buf()]
                oam = [sbt(ps, "oam%d" % i, [128, 512], BF16) for i in range(2)]
                oam_b = [Buf(), Buf()]

                def run_pipeline(tasks, hooks):
                    n = len(tasks)
                    for i in range(n + NST - 1):
                        if i in hooks:
                            for hfn in hooks[i]:
                                hfn()
                        for s_ in range(NST - 1, -1, -1):
                            t = i - s_
                            if 0 <= t < n:
                                tasks[t][s_]()

                def b_tasks(q_tile, q_buf, nq, k0, nkeys, vshift, vshift_b, oslot, oT_dst, c0, kTv, kTv_b, dvv, dvv_b):
                    tasks = []
                    per_head = [[] for _ in range(8)]
                    for h in range(8):
                        r0, hp = (h % 2) * 64, h // 2
                        obank = 4 + (h % 2)
                        C = {"prev": None}
                        ntile = (nkeys - k0 + 511) // 512
                        for ti in range(ntile):
                            start = k0 + ti * 512
                            W = min(512, nkeys - start)
                            ncb = (W + 127) // 128
                            T = {}

                            def s0(T=T, start=start, W=W, r0=r0, hp=hp, h=h):
                                zb = nxt("zb")
                                T["zb"] = zb
                                sc.op("pe", lambda e: e.matmul(
                                    bk(zb, W, nq), lhsT=q_tile[:, 4 + h, 0:nq], rhs=kTv[:, hp, start:start + W],
                                    start=True, stop=True), reads=[q_buf, kTv_b], writes=[bank_b[zb]])

                            def s1(T=T, W=W):
                                zb = T["zb"]
                                bs = nxt("p")
                                T["bs"] = bs
                                sc.op("act", lambda e: e.activation(
                                    out=beta[bs][0:nq, 0:W], in_=bk(zb, W, nq), func=AF.Sigmoid, scale=-1.0),
                                    reads=[bank_b[zb]], writes=[beta_b[bs], bank_b[zb]])

                            def s2(T=T, W=W, C=C, ti=ti, h=h, ntile=ntile):
                                bs = T["bs"]
                                pt = nxt("pt")
                                T["pt"] = pt
                                prev = C["prev"]
                                init = 1.0 if prev is None else cry[0:nq, prev:prev + 1]
                                d1 = trimask if ti == 0 else zeros
                                rd = [beta_b[bs], cbuf] + ([cry_b[prev]] if prev is not None else [])
                                sc.op("dve", lambda e: e.tensor_tensor_scan(
                                    out=Pt[pt][0:nq, 0:W], data0=beta[bs][0:nq, 0:W], data1=d1[0:nq, 0:W], initial=init,
                                    op0=ALU.mult, op1=ALU.max), reads=rd, writes=[Pt_b[pt]])
                                if ti < ntile - 1:
                                    cs = (h % 2) * 2 + (ti % 2)
                                    sc.op("pool", lambda e: e.tensor_copy(out=cry[0:nq, cs:cs + 1], in_=Pt[pt][0:nq, W - 1:W]),
                                          reads=[Pt_b[pt]], writes=[cry_b[cs]])
                                    C["prev"] = cs

                            def s3(T=T, W=W):
                                pt = T["pt"]

                                def src_fn(c, wc):
                                    return Pt[pt][0:nq, c * 128:c * 128 + wc]
                                src_fn.bufs = [Pt_b[pt], cbuf]
                                T["ts"], _ = transposes(src_fn, nq, W)

                            def s4(T=T, ncb=ncb, ti=ti):
                                pts = nxt("ss")
                                T["pts"] = pts
                                evacT(T["ts"], pT[pts], pT_b[pts], nq, ncb, ti == 0)

                            def s5(T=T, W=W, ncb=ncb, ti=ti, ntile=ntile, start=start, h=h):
                                pts = T["pts"]
                                obank = 4 + (h % 2)
                                for c in range(ncb):
                                    wc = min(128, W - c * 128)
                                    kb_ = start // 128 + c
                                    sc.op("pe", lambda e, c=c, wc=wc, kb_=kb_: e.matmul(
                                        bk(obank, 512, nq)[:, h * 64:(h + 1) * 64], lhsT=pT[pts][0:wc, c * 128:c * 128 + nq],
                                        rhs=dvv[0:wc, kb_, h * 64:(h + 1) * 64], start=(ti == 0 and c == 0),
                                        stop=(ti == ntile - 1 and c == ncb - 1)),
                                        reads=[pT_b[pts], dvv_b], writes=[bank_b[obank]])
                                if h == 7 and ti == ntile - 1:
                                    def par(ap, par_):
                                        return ap.rearrange("p (a b d) -> p a b d", b=2, d=64)[:, :, par_, :]
                                    for par_ in range(2):
                                        ob_ = 4 + par_
                                        sc.op("dve", lambda e, par_=par_, ob_=ob_: e.tensor_tensor(
                                            out=par(osb[oslot][0:nq, 0:512], par_), in0=par(bk(ob_, 512, nq), par_),
                                            in1=par(vshift[0:nq, :], par_), op=ALU.add),
                                            reads=[bank_b[ob_], vshift_b], writes=[osb_b[oslot], bank_b[ob_]])
                                    finish(osb[oslot], osb_b[oslot], nq, oT_dst, 2, c0)
                            per_head[h].append([s0, s1, s2, s3, s4, s5])
                    for p_ in range(4):
                        for ti in range(len(per_head[2 * p_])):
                            tasks.append(per_head[2 * p_][ti])
                            tasks.append(per_head[2 * p_ + 1][ti])
                    return tasks

                def sm_tasks(q_tile, q_buf, qbase, nq, kt_fn, kt_bufs, nk, v_fn, v_bufs, bias_t, col0, oslot, st, last,
                             oT_dst, c0):
                    tasks = []
                    obank = 5
                    ncb = (nk + 127) // 128
                    so = col0 // 64
                    for h in range(4):
                        r0, hp = (h % 2) * 64, h // 2
                        T = {}
                        pab = [bank_b[2], bank_b[3]]

                        def s0(T=T, r0=r0, hp=hp, h=h):
                            for c_ in range(0, nk, 512):
                                w = min(512, nk - c_)
                                sc.op("pe", lambda e, c_=c_, w=w: e.matmul(
                                    psB[0:nq, c_:c_ + w], lhsT=q_tile[:, qbase + h, 0:nq], rhs=kt_fn(r0, hp, c_, w),
                                    start=True, stop=True), reads=[q_buf] + kt_bufs, writes=pab)

                        def s1(T=T, h=h):
                            ss = nxt("ssb")
                            T["ss"] = ss
                            if bias_t is not None:
                                sc.op("dve", lambda e: e.tensor_tensor(
                                    out=ssb[ss][0:nq, 0:nk], in0=psB[0:nq, 0:nk], in1=bias_t[0:nq, h, 0:nk], op=ALU.add),
                                    reads=pab + [bias_b], writes=[ssb_b[ss]] + pab)
                            else:
                                sc.op("act", lambda e: e.copy(out=ssb[ss][0:nq, 0:nk], in_=psB[0:nq, 0:nk]),
                                      reads=pab, writes=[ssb_b[ss]] + pab)
                            sc.op("dve", lambda e: e.reduce_max(out=stt_[st][0:nq, 16 + so + h:17 + so + h],
                                                                in_=ssb[ss][0:nq, 0:nk], axis=AX.X),
                                  reads=[ssb_b[ss]], writes=[stt_b[st]])
                            sc.op("dve", lambda e: e.tensor_scalar(
                                out=stt_[st][0:nq, 24 + so + h:25 + so + h], in0=stt_[st][0:nq, 16 + so + h:17 + so + h],
                                scalar1=-1.0, scalar2=None, op0=ALU.mult), reads=[stt_b[st]], writes=[stt_b[st]])

                        def s2(T=T, h=h):
                            ss = T["ss"]
                            pt = nxt("pt")
                            T["pt"] = pt
                            sc.op("act", lambda e: e.activation(
                                out=Pt[pt][0:nq, 0:nk], in_=ssb[ss][0:nq, 0:nk], func=AF.Exp,
                                bias=stt_[st][0:nq, 24 + so + h:25 + so + h], scale=1.0,
                                accum_out=stt_[st][0:nq, so + h:so + h + 1]), reads=[ssb_b[ss], stt_b[st]],
                                writes=[Pt_b[pt], stt_b[st]])

                        def s3(T=T):
                            pt = T["pt"]

                            def src_fn(c, wc):
                                return Pt[pt][0:nq, c * 128:c * 128 + wc]
                            src_fn.bufs = [Pt_b[pt], cbuf]
                            T["ts"], _ = transposes(src_fn, nq, nk)

                        def s4(T=T):
                            pts = nxt("ss")
                            T["pts"] = pts
                            evacT(T["ts"], pT[pts], pT_b[pts], nq, ncb, False)

                        def s5(T=T, h=h):
                            pts = T["pts"]
                            for c in range(ncb):
                                wc = min(128, nk - c * 128)
                                sc.op("pe", lambda e, c=c, wc=wc: e.matmul(
                                    bk(obank, 512, nq)[:, col0 + h * 64:col0 + (h + 1) * 64],
                                    lhsT=pT[pts][0:wc, c * 128:c * 128 + nq], rhs=v_fn(c, wc, h),
                                    start=(c == 0), stop=(c == ncb - 1)),
                                    reads=[pT_b[pts]] + v_bufs, writes=[bank_b[obank]])
                            if last and h == 3:
                                sc.op("dve", lambda e: e.reciprocal(out=stt_[st][0:nq, 8:16], in_=stt_[st][0:nq, 0:8]),
                                      reads=[stt_b[st]], writes=[stt_b[st]])
                                for hh in range(8):
                                    sc.op("dve", lambda e, hh=hh: e.tensor_scalar(
                                        out=oam[oslot][0:nq, hh * 64:(hh + 1) * 64],
                                        in0=bk(obank, 512, nq)[:, hh * 64:(hh + 1) * 64], scalar1=stt_[st][0:nq, 8 + hh:9 + hh],
                                        scalar2=None, op0=ALU.mult), reads=[bank_b[obank], stt_b[st]],
                                        writes=[oam_b[oslot], bank_b[obank]])
                                finish(oam[oslot], oam_b[oslot], nq, oT_dst, 0, c0)
                        tasks.append([s0, s1, s2, s3, s4, s5])
                    return tasks

                rr.update({"zb": 0, "ssb": 0, "ot": 0})

                def finish(src, src_b, nq, oT_dst, which, c0):
                    os_ = nxt("ot")

                    def src_fn(c, wc):
                        return src[0:nq, c * 128:c * 128 + 128]
                    src_fn.bufs = [src_b, cbuf]
                    ts, ncb = transposes(src_fn, nq, 512)
                    src3 = psT[ts][:, 0:512].rearrange("p (c q) -> p c q", q=128)
                    evac(oT[os_][:, 0:4, 0:nq], src3[:, :, 0:nq], [psT_b[ts]], [oT_b[os_]], eng="act")
                    d3 = oT_dst.rearrange("(f p) t -> p f t", p=128)
                    if which == 2:
                        sc.dma("sp", d3[:, 2:6, c0:c0 + nq], oT[os_][:, 0:4, 0:nq], reads=[oT_b[os_]])
                    else:
                        sc.dma("sp", d3[:, 0:2, c0:c0 + nq], oT[os_][:, 0:2, 0:nq], reads=[oT_b[os_]])
                        sc.dma("sp", d3[:, 6:8, c0:c0 + nq], oT[os_][:, 2:4, 0:nq], reads=[oT_b[os_]])

                if DBG & 1:
                    proj3 = projT_p.rearrange("(f p) t -> p f t", p=128)
                    for t4 in range(4):
                        sc.dma("sp", kT[:, t4, :], projT_p[(8 + t4) * 128:(9 + t4) * 128, :], writes=[kT_b])
                    build_dv(lambda j0, m: [(0, m, pbv_d[j0:j0 + m, :])], S, dv, dv_b)
                    sc.dma("sp", mkT[:], mkT_d.rearrange("(f p) t -> p f t", p=128), writes=[mk_b])
                    sc.dma("sp", mv[:], mv_d.rearrange("(r p) f -> p r f", p=128), writes=[mk_b])
                    nblocks = DBG_NTOK // 128
                    tasks, hooks = [], {}

                    def loads(jb):
                        sl = jb % 2
                        c0 = jb * 128
                        nkA = min(640, S - c0)
                        load_q(sl, proj3, c0, 128)
                        sc.dma("sp", vsh[sl][:], pbv_d[c0 + 1:c0 + 129, :], writes=[vsh_b[sl]])
                        sc.dma("sp", kaT[sl][:, :, 0:nkA], proj3[:, 2:4, c0:c0 + nkA], writes=[kaT_b[sl]])
                        sc.dma("sp", vaw[sl][:, 0:nkA // 128, :],
                               va_p[c0:c0 + nkA, :].rearrange("(r p) f -> p r f", p=128), writes=[vaw_b[sl]])

                    for jb in range(nblocks):
                        sl = jb % 2
                        c0 = jb * 128
                        nkA = min(640, S - c0)
                        first = len(tasks)
                        if jb == 0:
                            hooks.setdefault(0, []).append(lambda: loads(0))
                            if nblocks > 1:
                                hooks.setdefault(0, []).append(lambda: loads(1))
                        elif jb + 1 < nblocks:
                            hooks.setdefault(first + NST, []).append(lambda jb=jb: loads(jb + 1))
                        tasks += sm_tasks(qt[sl], qt_b[sl], 0, 128,
                                          lambda r0, hp, c_, w, sl=sl: kaT[sl][:, hp, c_:c_ + w],
                                          [kaT_b[sl]], nkA, lambda c, wc, h, sl=sl: vaw[sl][0:wc, c, h * 64:(h + 1) * 64],
                                          [vaw_b[sl]], biasp, 0, sl, sl, False, oT_p, c0)
                        tasks += sm_tasks(qt[sl], qt_b[sl], 12, 128, lambda r0, hp, c_, w: mkT[:, hp, c_:c_ + w],
                                          [mk_b], 256, lambda c, wc, h: mv[0:wc, c, h * 64:(h + 1) * 64], [mk_b], None,
                                          256, sl, sl, True, oT_p, c0)
                        tasks += b_tasks(qt[sl], qt_b[sl], 128, c0, S, vsh[sl], vsh_b[sl], sl, oT_p, c0, kT, kT_b, dv, dv_b)
                    run_pipeline(tasks, hooks)

                if DBG & 2:
                    sc.barrier()
                    proj3s = projT_s.rearrange("(f p) t -> p f t", p=128)
                    biass = biasp
                    sc.dma("sp", biasp[0:16, :, 0:528], bias_s.rearrange("h q k -> q h k"), writes=[bias_b])

                    def prep(b):
                        sl = b % 2
                        c0 = b * 16

                        def vr_pieces(j0, m):
                            out = []
                            j, end = j0, j0 + m
                            while j < end:
                                if j < 16:
                                    e_ = min(end, 16)
                                    out.append((j - j0, e_ - j, sbv_d[c0 + j:c0 + e_, :]))
                                elif j < NKS:
                                    e_ = min(end, NKS)
                                    out.append((j - j0, e_ - j, cbv[b, j - 16:e_ - 16, :]))
                                else:
                                    e_ = end
                                    out.append((j - j0, 1, pbv_d[S:S + 1, :]))
                                j = e_
                            return out

                        load_q(sl, proj3s, c0, 16)
                        sc.dma("sp", kTs[sl][:, :, 0:16], proj3s[:, 8:12, c0:c0 + 16], writes=[kTs_b[sl]])
                        dvi = [0]
                        ndvb = (NKS + 127) // 128
                        for g in range(0, 32, 2):
                            sc.dma("sp", c2, cbk[b, g * 128:(g + 2) * 128, :].rearrange("(r p) f -> p r f", p=128),
                                   writes=[cst_b])
                            for hp in range(4):
                                zb = nxt("z", 4)
                                for i in range(2):
                                    sc.op("pe", lambda e, i=i, hp=hp, zb=zb: e.transpose(
                                        out=bk(zb)[:, i * 128:(i + 1) * 128], in_=c2[:, i, hp * 128:(hp + 1) * 128],
                                        identity=identf[:]), reads=[cst_b, cbuf], writes=[bank_b[zb]])
                                evac(kTs[sl][:, hp, 16 + g * 128:16 + (g + 2) * 128], bk(zb, 256), [bank_b[zb]], [kTs_b[sl]])
                            for _ in range(2):
                                if dvi[0] < ndvb:
                                    dv_block(vr_pieces, NKS, dvs[sl], dvs_b[sl], dvi[0])
                                    dvi[0] += 1
                        while dvi[0] < ndvb:
                            dv_block(vr_pieces, NKS, dvs[sl], dvs_b[sl], dvi[0])
                            dvi[0] += 1
                        for (d0, nr, src) in vr_pieces(1, 16):
                            sc.dma("sp", vsh[sl][d0:d0 + nr, :], src, writes=[vsh_b[sl]])
                        sc.dma("sp", kaT[sl][:, :, 512:528], proj3s[:, 2:4, c0:c0 + 16], writes=[kaT_b[sl]])
                        sc.dma("sp", c4, cak[b].rearrange("(r p) f -> p r f", p=128), writes=[cst_b])
                        for hp in range(2):
                            zb = nxt("z", 4)
                            for i in range(4):
                                sc.op("pe", lambda e, i=i, hp=hp, zb=zb: e.transpose(
                                    out=bk(zb)[:, i * 128:(i + 1) * 128], in_=c4[:, i, hp * 128:(hp + 1) * 128],
                                    identity=identf[:]), reads=[cst_b, cbuf], writes=[bank_b[zb]])
                            evac(kaT[sl][:, hp, 0:512], bk(zb), [bank_b[zb]], [kaT_b[sl]])
                        sc.dma("sp", c4, cav[b].rearrange("(r p) f -> p r f", p=128), writes=[cst_b])
                        sc.op("pool", lambda e, sl=sl: e.tensor_copy(out=vaw[sl][:, 0:4, :], in_=c4),
                              reads=[cst_b], writes=[vaw_b[sl]])
                        sc.dma("sp", vaw[sl][0:16, 4, :], va_s[c0:c0 + 16, :], writes=[vaw_b[sl]])
                        sc.dma("sp", c4[:, 0:2, :], cmk[b].rearrange("(r p) f -> p r f", p=128), writes=[cst_b])
                        for hp in range(2):
                            zb = nxt("z", 4)
                            for i in range(2):
                                sc.op("pe", lambda e, i=i, hp=hp, zb=zb: e.transpose(
                                    out=bk(zb)[:, i * 128:(i + 1) * 128], in_=c4[:, i, hp * 128:(hp + 1) * 128],
                                    identity=identf[:]), reads=[cst_b, cbuf], writes=[bank_b[zb]])
                            evac(mkTs[sl][:, hp, :], bk(zb, 256), [bank_b[zb]], [mks_b[sl]])
                        sc.dma("sp", c4[:, 2:4, :], cmv[b].rearrange("(r p) f -> p r f", p=128), writes=[cst_b])
                        sc.op("pool", lambda e, sl=sl: e.tensor_copy(out=mvs[sl][:], in_=c4[:, 2:4, :]),
                              reads=[cst_b], writes=[mks_b[sl]])

                    def run_tasks(b):
                        sl = b % 2
                        c0 = b * 16
                        tasks = sm_tasks(qt[sl], qt_b[sl], 0, 16, lambda r0, hp, c_, w: kaT[sl][:, hp, c_:c_ + w],
                                         [kaT_b[sl]], 528, lambda c, wc, h: vaw[sl][0:wc, c, h * 64:(h + 1) * 64],
                                         [vaw_b[sl]], biass, 0, sl, sl, False, oT_s, c0)
                        tasks += sm_tasks(qt[sl], qt_b[sl], 12, 16, lambda r0, hp, c_, w: mkTs[sl][:, hp, c_:c_ + w],
                                          [mks_b[sl]], 256, lambda c, wc, h: mvs[sl][0:wc, c, h * 64:(h + 1) * 64],
                                          [mks_b[sl]], None, 256, sl, sl, True, oT_s, c0)
                        tasks += b_tasks(qt[sl], qt_b[sl], 16, 0, NKS, vsh[sl], vsh_b[sl], sl, oT_s, c0,
                                         kTs[sl], kTs_b[sl], dvs[sl], dvs_b[sl])
                        run_pipeline(tasks, {})

                    prep(0)
                    prep(1)
                    run_tasks(0)
                    prep(2)
                    run_tasks(1)
                    prep(3)
                    run_tasks(2)
                    run_tasks(3)
                sc.barrier()

        phase2()
        if STOP <= 2:
            sc.barrier()
            return nc

        def layer_norm(ps_tiles, r, r_b, m, g_t, b_t, out, out_b):
            st6, mvt, st_b = ps_tiles
            for hf in range(2):
                sc.op("dve", lambda e, hf=hf: e.bn_stats(out=st6[0:m, hf, :], in_=r[0:m, hf * 512:(hf + 1) * 512]),
                      reads=[r_b], writes=[st_b])
            sc.op("dve", lambda e: e.bn_aggr(out=mvt[0:m, 0:2], in_=st6[0:m, :, :].rearrange("p a b -> p (a b)")),
                  reads=[st_b], writes=[st_b])
            sc.op("dve", lambda e: e.tensor_scalar(out=mvt[0:m, 2:3], in0=mvt[0:m, 1:2], scalar1=EPS, scalar2=None,
                                                   op0=ALU.add), reads=[st_b], writes=[st_b])
            sc.op("act", lambda e: e.activation(out=mvt[0:m, 3:4], in_=mvt[0:m, 2:3], func=AF.Sqrt),
                  reads=[st_b], writes=[st_b])
            sc.op("dve", lambda e: e.reciprocal(out=mvt[0:m, 4:5], in_=mvt[0:m, 3:4]), reads=[st_b], writes=[st_b])
            sc.op("dve", lambda e: e.tensor_scalar(out=out[0:m, :], in0=r[0:m, :], scalar1=mvt[0:m, 0:1],
                                                   scalar2=mvt[0:m, 4:5], op0=ALU.subtract, op1=ALU.mult),
                  reads=[r_b, st_b], writes=[out_b])
            sc.op("pool", lambda e: e.tensor_tensor(out=out[0:m, :], in0=out[0:m, :], in1=g_t[0:m, :], op=ALU.mult),
                  reads=[out_b, cbuf], writes=[out_b])
            sc.op("pool", lambda e: e.tensor_tensor(out=out[0:m, :], in0=out[0:m, :], in1=b_t[0:m, :], op=ALU.add),
                  reads=[out_b, cbuf], writes=[out_b])

        def phase3a():
            with ExitStack() as ps:
                stage = [sbt(ps, "wst%d" % i, [128, 1024], F32) for i in range(3)]
                stage_b = [Buf(), Buf(), Buf()]
                wg, wg_b = load_weight_bf(ps, "wg", w_gate, D, 3 * D, stage, stage_b)
                wp = sbt(ps, "wp", [128, 8, D], BF16)
                wp_b = Buf()
                i = 0
                for (wap, nk, k0) in ((w_pa, 2, 0), (w_pb, 4, 2), (w_pm, 2, 6)):
                    for k in range(nk):
                        st, stb = stage[i % 2], stage_b[i % 2]
                        sc.dma("sp", st[:], wap[k * 128:(k + 1) * 128, :], writes=[stb])
                        sc.op("pool", lambda e, st=st, kk=k0 + k: e.tensor_copy(out=wp[:, kk, :], in_=st[:]),
                              reads=[stb], writes=[wp_b])
                        i += 1
                wo, wo_b = load_weight_bf(ps, "wo", w_o, D, D, stage, stage_b)
                bg = sbt(ps, "bg", [128, 24], F32)
                lg = sbt(ps, "lg", [128, D], F32)
                lb = sbt(ps, "lb", [128, D], F32)
                sc.dma("sp", bg[:], b_gateT, writes=[cbuf])
                sc.dma("sp", lg[:], ln1_g, writes=[cbuf])
                sc.dma("sp", lb[:], ln1_b, writes=[cbuf])
                xb = [sbt(ps, "xb%d" % i, [128, KD, 512], BF16) for i in range(2)]
                xb_b = [Buf(), Buf()]
                ot = [sbt(ps, "ot%d" % i, [128, 8, 512], BF16) for i in range(2)]
                ot_b = [Buf(), Buf()]
                gs = [sbt(ps, "gs%d" % i, [128, 512], F32) for i in range(2)]
                gs_b = [Buf(), Buf()]
                tmp2 = [[sbt(ps, "tmp%d_%d" % (i, j), [128, 512], F32) for i in range(3)] for j in range(2)]
                tmp2_b = [[Buf(), Buf(), Buf()] for j in range(2)]
                hT = sbt(ps, "hT", [128, 8, 512], BF16)
                hT_b = Buf()
                xt = [sbt(ps, "xt%d" % i, [128, D], F32) for i in range(2)]
                xt_b = [Buf(), Buf()]
                r1 = [sbt(ps, "r1%d" % i, [128, D], F32) for i in range(2)]
                r1_b = [Buf(), Buf()]
                h1 = [sbt(ps, "h1%d" % i, [128, D], F32) for i in range(2)]
                h1_b = [Buf(), Buf()]
                h1bf = sbt(ps, "h1bf", [128, D], BF16)
                h1bf_b = Buf()
                h1t = [sbt(ps, "h1t%d" % i, [128, 8, 128], BF16) for i in range(2)]
                h1t_b = [Buf(), Buf()]
                st6 = sbt(ps, "st6", [128, 2, 6], F32)
                mvt = sbt(ps, "mvt", [128, 8], F32)
                lnt = (st6, mvt, Buf())
                rr = {"z": 0, "t": 0, "blk": 0}

                def nb():
                    v = rr["z"] % 6
                    rr["z"] += 1
                    return v

                h1bf2 = sbt(ps, "h1bf2", [128, D], BF16)
                h1bfs = [h1bf, h1bf2]
                h1bfs_b = [h1bf_b, Buf()]
                deferred = []

                def flush():
                    while deferred:
                        deferred.pop(0)()

                def run(ntok, xbf_scr, oT_scr, x_tm, h1_scr, h1T_scr):
                    xbf3 = xbf_scr.rearrange("(k p) t -> p k t", p=128)
                    oT3 = oT_scr.rearrange("(k p) t -> p k t", p=128)
                    h1T3 = h1T_scr.rearrange("(k p) t -> p k t", p=128)
                    ng = (ntok + 511) // 512

                    def gload(g):
                        sl = g % 2
                        c0 = g * 512
                        n = min(512, ntok - c0)
                        sc.dma("sp", xb[sl][:, :, 0:n], xbf3[:, :, c0:c0 + n], writes=[xb_b[sl]])
                        sc.dma("sp", ot[sl][:, :, 0:n], oT3[:, :, c0:c0 + n], writes=[ot_b[sl]])

                    def xload(r0, m, ts):
                        sc.dma("sp", xt[ts][0:m, :], x_tm[r0:r0 + m, :], writes=[xt_b[ts]])

                    gload(0)
                    for g in range(ng):
                        sl = g % 2
                        c0 = g * 512
                        n = min(512, ntok - c0)
                        if g + 1 < ng:
                            gload(g + 1)
                        nblk = (n + 127) // 128
                        ts0 = rr["blk"]
                        xload(c0, min(128, n), ts0 % 2)
                        for ft in range(8):
                            tmp, tmp_b = tmp2[ft % 2], tmp2_b[ft % 2]
                            for br, (k0, nk) in enumerate(((0, 2), (2, 4), (6, 2))):
                                bi = nb()
                                for k in range(KD):
                                    sc.op("pe", lambda e, k=k, bi=bi, br=br: e.matmul(
                                        bk(bi, n), lhsT=wg[:, k, br * D + ft * 128:br * D + ft * 128 + 128],
                                        rhs=xb[sl][:, k, 0:n], start=(k == 0), stop=(k == KD - 1)),
                                        reads=[wg_b, xb_b[sl]], writes=[bank_b[bi]])
                                gsl = rr["t"] % 2
                                rr["t"] += 1
                                sc.op("act", lambda e, bi=bi, gsl=gsl, br=br: e.activation(
                                    out=gs[gsl][:, 0:n], in_=bk(bi, n), func=AF.Sigmoid,
                                    bias=bg[:, br * 8 + ft:br * 8 + ft + 1], scale=1.0),
                                    reads=[bank_b[bi], cbuf], writes=[gs_b[gsl], bank_b[bi]])
                                b2 = nb()
                                for kk in range(nk):
                                    sc.op("pe", lambda e, kk=kk, b2=b2, k0=k0, nk=nk: e.matmul(
                                        bk(b2, n), lhsT=wp[:, k0 + kk, ft * 128:(ft + 1) * 128], rhs=ot[sl][:, k0 + kk, 0:n],
                                        start=(kk == 0), stop=(kk == nk - 1)),
                                        reads=[wp_b, ot_b[sl]], writes=[bank_b[b2]])
                                sc.op("dve", lambda e, b2=b2, gsl=gsl, br=br: e.tensor_tensor(
                                    out=tmp[br][:, 0:n], in0=bk(b2, n), in1=gs[gsl][:, 0:n], op=ALU.mult),
                                    reads=[bank_b[b2], gs_b[gsl]], writes=[tmp_b[br], bank_b[b2]])
                            sc.op("pool", lambda e: e.tensor_tensor(out=tmp[0][:, 0:n], in0=tmp[0][:, 0:n], in1=tmp[1][:, 0:n],
                                                                    op=ALU.add), reads=[tmp_b[0], tmp_b[1]], writes=[tmp_b[0]])
                            sc.op("pool", lambda e, ft=ft: e.tensor_tensor(out=hT[:, ft, 0:n], in0=tmp[0][:, 0:n],
                                                                           in1=tmp[2][:, 0:n], op=ALU.add),
                                  reads=[tmp_b[0], tmp_b[2]], writes=[hT_b])
                            if ft == 0:
                                flush()
                        for tb in range(nblk):
                            m = min(128, n - tb * 128)
                            r0 = c0 + tb * 128
                            ts = rr["blk"] % 2
                            rr["blk"] += 1
                            banks2 = []
                            for hf in range(2):
                                bi = nb()
                                banks2.append(bi)
                                for ft in range(8):
                                    sc.op("pe", lambda e, ft=ft, bi=bi, hf=hf: e.matmul(
                                        bk(bi, 512, m), lhsT=hT[:, ft, tb * 128:tb * 128 + m],
                                        rhs=wo[:, ft, hf * 512:(hf + 1) * 512], start=(ft == 0), stop=(ft == 7)),
                                        reads=[wo_b, hT_b], writes=[bank_b[bi]])
                            flush()
                            if tb + 1 < nblk:
                                xload(r0 + 128, min(128, n - (tb + 1) * 128), (ts + 1) % 2)
                            for hf in range(2):
                                bi = banks2[hf]
                                sc.op("dve", lambda e, bi=bi, hf=hf: e.scalar_tensor_tensor(
                                    out=r1[ts][0:m, hf * 512:(hf + 1) * 512], in0=xt[ts][0:m, hf * 512:(hf + 1) * 512],
                                    scalar=ALPHA, in1=bk(bi, 512, m), op0=ALU.mult, op1=ALU.add),
                                    reads=[xt_b[ts], bank_b[bi]], writes=[r1_b[ts], bank_b[bi]])
                            layer_norm(lnt, r1[ts], r1_b[ts], m, lg, lb, h1[ts], h1_b[ts])
                            sc.dma("sp", h1_scr[r0:r0 + m, :], h1[ts][0:m, :], reads=[h1_b[ts]])
                            sc.op("act", lambda e, ts=ts, m=m: e.copy(out=h1bfs[ts][0:m, :], in_=h1[ts][0:m, :]),
                                  reads=[h1_b[ts]], writes=[h1bfs_b[ts]])

                            def tail(ts=ts, m=m, r0=r0):
                                for hf in range(2):
                                    tsl = rr["t"] % 2
                                    rr["t"] += 1
                                    for c in range(4):
                                        sc.op("pe", lambda e, c=c, hf=hf, tsl=tsl: e.transpose(
                                            out=psT[tsl][:, c * 128:c * 128 + m],
                                            in_=h1bfs[ts][0:m, hf * 512 + c * 128:hf * 512 + (c + 1) * 128],
                                            identity=ident[0:m, 0:m]), reads=[h1bfs_b[ts], cbuf], writes=[psT_b[tsl]])
                                    src3 = psT[tsl][:, 0:512].rearrange("p (c q) -> p c q", q=128)
                                    evac(h1t[ts][:, hf * 4:hf * 4 + 4, 0:m], src3[:, :, 0:m], [psT_b[tsl]], [h1t_b[ts]])
                                sc.dma("sp", h1T3[:, :, r0:r0 + m], h1t[ts][:, :, 0:m], reads=[h1t_b[ts]])
                            deferred.append(tail)
                    flush()

                if DBG & 1:
                    run(DBG_NTOK, xbf_p, oT_p, xr, h1_p, h1T_p)
                if DBG & 2:
                    run(NS, xbf_s, oT_s, xs, h1_s, h1T_s)
                sc.barrier()

        phase3a()
        if STOP <= 3:
            sc.barrier()
            return nc

        def phase3b():
            with ExitStack() as ps:
                stage = [sbt(ps, "wst%d" % i, [128, 1024], F32) for i in range(2)]
                stage_b = [Buf(), Buf()]
                wup = sbt(ps, "wup", [128, KD, 2 * DFF], BF16)
                wup_pb = [Buf() for _ in range(6)]
                wdn = sbt(ps, "wdn", [128, NFT, D], BF16)
                wdn_b = Buf()
                ld_i = [0]

                def load_chunk(w_ap, r0, c0, c1, dst_ap, dst_buf):
                    i = ld_i[0]
                    ld_i[0] += 1
                    st, stb = stage[i % len(stage)], stage_b[i % len(stage)]
                    sc.dma("sp", st[:, 0:c1 - c0], w_ap[r0:r0 + 128, c0:c1], writes=[stb])
                    ce = ("pool", "dve", "act")[i % 3]
                    if ce == "act":
                        sc.op(ce, lambda e: e.copy(out=dst_ap, in_=st[:, 0:c1 - c0]), reads=[stb], writes=[dst_buf])
                    else:
                        sc.op(ce, lambda e: e.tensor_copy(out=dst_ap, in_=st[:, 0:c1 - c0]), reads=[stb], writes=[dst_buf])

                def load_up_piece(pc):
                    c0 = pc * 1024
                    c1 = min(2 * DFF, c0 + 1024)
                    for k in range(KD):
                        load_chunk(w_up, k * 128, c0, c1, wup[:, k, c0:c1], wup_pb[pc])

                def load_dn(k0, k1):
                    for k in range(k0, k1):
                        load_chunk(w_down, k * 128, 0, D, wdn[:, k, :], wdn_b)

                load_up_piece(0)
                load_up_piece(2)
                pending = [lambda: load_up_piece(3), lambda: load_up_piece(1), lambda: load_up_piece(4),
                           lambda: load_up_piece(5), lambda: load_dn(0, 6), lambda: load_dn(6, 12),
                           lambda: load_dn(12, 18), lambda: load_dn(18, NFT)]
                cw = sbt(ps, "cw", [128, 2 * NFT, 3], F32)
                cb = sbt(ps, "cb", [128, 2 * NFT], F32)
                lg = sbt(ps, "lg", [128, D], F32)
                lb = sbt(ps, "lb", [128, D], F32)
                stt = sbt(ps, "stt", [128, 2 * NFT, 4, 2], F32)
                sco = sbt(ps, "sco", [128, 2 * NFT, 4, 2], F32)
                carry = sbt(ps, "carry", [128, 2 * NFT, 2], F32)
                carry_b, sco_b = Buf(), Buf()
                sc.dma("sp", cw[:], conv_wT, writes=[cbuf])
                sc.dma("sp", cb[:], conv_bT, writes=[cbuf])
                sc.dma("sp", lg[:], ln2_g, writes=[cbuf])
                sc.dma("sp", lb[:], ln2_b, writes=[cbuf])
                sc.dma("sp", stt[:], stT, writes=[cbuf])
                sc.op("pool", lambda e: e.memset(carry[:], 0.0), writes=[carry_b])
                GW = 256
                hT = [sbt(ps, "hT%d" % i, [128, KD, GW], BF16) for i in range(2)]
                hT_b = [Buf(), Buf()]
                NEX = 7
                ext = [sbt(ps, "ext%d" % i, [128, GW + 8], F32) for i in range(NEX)]
                ext_b = [Buf() for _ in range(NEX)]
                exth_b = [Buf() for _ in range(NEX)]
                cc = [sbt(ps, "cc%d" % i, [128, GW], F32) for i in range(NEX)]
                cc_b = [Buf() for _ in range(NEX)]
                aT = sbt(ps, "aT", [128, NFT, GW], BF16)
                aT_b = Buf()
                ht = [sbt(ps, "ht%d" % i, [128, D], F32) for i in range(2)]
                ht_b = [Buf(), Buf()]
                r2 = [sbt(ps, "r2%d" % i, [128, D], F32) for i in range(2)]
                r2_b = [Buf(), Buf()]
                st6 = sbt(ps, "st6", [128, 2, 6], F32)
                mvt = sbt(ps, "mvt", [128, 8], F32)
                lnt = (st6, mvt, Buf())
                rr = {"z": 0, "e": 0, "blk": 0, "g": 0}

                def nb():
                    v = rr["z"] % 6
                    rr["z"] += 1
                    return v

                def hload(h1T3, c0, n, slot):
                    sc.dma("sp", hT[slot][:, :, 0:n], h1T3[:, :, c0:c0 + n], writes=[hT_b[slot]])

                def group(h1T3, c0, n, nseg, L, h1_scr, y_out, sample, pre=None):
                    sl = rr["g"] % 2
                    rr["g"] += 1
                    if pre is not None:
                        pre((sl + 1) % 2)
                    nblk_ = (n + 127) // 128
                    for tb in range(nblk_):
                        m_ = min(128, n - tb * 128)
                        sc.dma("sp", ht[(rr["blk"] + tb) % 2][0:m_, :], h1_scr[c0 + tb * 128:c0 + tb * 128 + m_, :],
                               writes=[ht_b[(rr["blk"] + tb) % 2]])
                    def stA(fft):
                        info = []
                        for part in range(2):
                            idx = part * NFT + fft
                            col = idx * 128
                            bi = nb()
                            for k in range(KD):
                                sc.op("pe", lambda e, k=k, bi=bi, col=col: e.matmul(
                                    bk(bi, n), lhsT=wup[:, k, col:col + 128], rhs=hT[sl][:, k, 0:n],
                                    start=(k == 0), stop=(k == KD - 1)), reads=[wup_pb[col // 1024], hT_b[sl]],
                                    writes=[bank_b[bi]])
                            es_ = rr["e"] % NEX
                            rr["e"] += 1
                            e3 = ext[es_][:, 0:nseg * (L + 2)].rearrange("p (s l) -> p s l", l=L + 2)
                            if sample:
                                sc.op("pool", lambda e, e3=e3, idx=idx: e.tensor_copy(out=e3[:, :, L:L + 2], in_=stt[:, idx, :, :]),
                                      reads=[cbuf], writes=[exth_b[es_]])
                            else:
                                sc.op("pool", lambda e, e3=e3, idx=idx: e.tensor_copy(out=e3[:, 0, L:L + 2], in_=carry[:, idx, :]),
                                      reads=[carry_b], writes=[exth_b[es_]])
                            sc.op("act", lambda e, bi=bi, e3=e3: e.copy(
                                out=e3[:, :, 0:L], in_=bk(bi, n).rearrange("p (s l) -> p s l", l=L)),
                                reads=[bank_b[bi]], writes=[ext_b[es_], bank_b[bi]])
                            if sample:
                                sc.op("pool", lambda e, e3=e3, idx=idx: e.tensor_copy(out=sco[:, idx, :, :], in_=e3[:, :, 0:2]),
                                      reads=[ext_b[es_]], writes=[sco_b])
                            else:
                                sc.op("pool", lambda e, e3=e3, idx=idx: e.tensor_copy(out=carry[:, idx, :], in_=e3[:, 0, 0:2]),
                                      reads=[ext_b[es_]], writes=[carry_b])
                            c3 = cc[es_][:, 0:n].rearrange("p (s l) -> p s l", l=L)
                            sc.op("act", lambda e, bi=bi, c3=c3, idx=idx: e.activation(
                                out=c3, in_=bk(bi, n).rearrange("p (s l) -> p s l", l=L), func=AF.Identity,
                                bias=cb[:, idx:idx + 1], scale=cw[:, idx, 2:3]),
                                reads=[bank_b[bi], cbuf], writes=[cc_b[es_], bank_b[bi]])
                            info.append((es_, e3, c3, idx))
                        return info

                    def stB(info):
                        for j in (1, 2):
                            for (es_, e3, c3, idx) in info:
                                sc.op("dve", lambda e, e3=e3, c3=c3, idx=idx, j=j: e.scalar_tensor_tensor(
                                    out=c3, in0=e3[:, :, j:L + j], scalar=cw[:, idx, 2 - j:3 - j], in1=c3,
                                    op0=ALU.mult, op1=ALU.add), reads=[ext_b[es_], exth_b[es_], cbuf, cc_b[es_]],
                                    writes=[cc_b[es_]])

                    def stC(info, fft):
                        g_, v_ = info[0][0], info[1][0]
                        sc.op("act", lambda e: e.activation(out=cc[g_][:, 0:n], in_=cc[g_][:, 0:n], func=AF.Silu),
                              reads=[cc_b[g_]], writes=[cc_b[g_]])
                        sc.op("pool", lambda e: e.tensor_tensor(
                            out=aT[:, fft, 0:n], in0=cc[g_][:, 0:n], in1=cc[v_][:, 0:n], op=ALU.mult),
                            reads=[cc_b[g_], cc_b[v_]], writes=[aT_b])

                    infos = {}
                    for it in range(NFT + 2):
                        if pending:
                            pending.pop(0)()
                        if it < NFT:
                            infos[it] = stA(it)
                        if 0 <= it - 1 < NFT:
                            stB(infos[it - 1])
                        if 0 <= it - 2 < NFT:
                            stC(infos[it - 2], it - 2)
                    for tb in range((n + 127) // 128):
                        m = min(128, n - tb * 128)
                        r0 = c0 + tb * 128
                        ts = rr["blk"] % 2
                        rr["blk"] += 1
                        for hf in range(2):
                            bi = nb()
                            for fft in range(NFT):
                                sc.op("pe", lambda e, fft=fft, bi=bi, hf=hf: e.matmul(
                                    bk(bi, 512, m), lhsT=aT[:, fft, tb * 128:tb * 128 + m],
                                    rhs=wdn[:, fft, hf * 512:(hf + 1) * 512], start=(fft == 0), stop=(fft == NFT - 1)),
                                    reads=[wdn_b, aT_b], writes=[bank_b[bi]])
                            sc.op("dve", lambda e, bi=bi, hf=hf: e.scalar_tensor_tensor(
                                out=r2[ts][0:m, hf * 512:(hf + 1) * 512], in0=ht[ts][0:m, hf * 512:(hf + 1) * 512],
                                scalar=ALPHA, in1=bk(bi, 512, m), op0=ALU.mult, op1=ALU.add),
                                reads=[ht_b[ts], bank_b[bi]], writes=[r2_b[ts], bank_b[bi]])
                        layer_norm(lnt, r2[ts], r2_b[ts], m, lg, lb, ht[ts], ht_b[ts])
                        sc.dma("sp", y_out[r0:r0 + m, :], ht[ts][0:m, :], reads=[ht_b[ts]])

                if DBG & 1:
                    h1T3 = h1T_p.rearrange("(k p) t -> p k t", p=128)
                    ngp = DBG_NTOK // GW
                    hload(h1T3, (ngp - 1) * GW, GW, rr["g"] % 2)
                    for g in range(ngp - 1, -1, -1):
                        pre = (lambda slot, g=g: hload(h1T3, (g - 1) * GW, GW, slot)) if g > 0 else None
                        group(h1T3, g * GW, GW, 1, GW, h1_p, yp_d, False, pre)
                    sc.dma("sp", pconv_d, carry[:], reads=[carry_b])
                if DBG & 2:
                    h1T3 = h1T_s.rearrange("(k p) t -> p k t", p=128)
                    hload(h1T3, 0, NS, rr["g"] % 2)
                    group(h1T3, 0, NS, 4, 16, h1_s, ys_d, True)
                    sc.dma("sp", sconv_d, sco[:], reads=[sco_b])
                sc.barrier()

        phase3b()

        sc.barrier()
    return nc


def _prep_inputs(inp, c):
    f = np.float32
    x = np.asarray(inp["x_prompt"][c], dtype=f)
    xr_ = np.ascontiguousarray(x[::-1])
    xrT_ = np.zeros((D, S + 1), dtype=f)
    xrT_[:, :S] = xr_.T
    bsl = slice(4 * c, 4 * c + 4)
    xs_ = np.asarray(inp["x_sample"][bsl], dtype=f)[:, ::-1, :].reshape(NS, D)
    m = {}
    m["xrT"] = xrT_
    m["xr"] = xr_
    m["memT"] = np.ascontiguousarray(np.asarray(inp["mem_prompt"][c], dtype=f).T)
    m["xsT"] = np.ascontiguousarray(xs_.T)
    m["xs"] = np.ascontiguousarray(xs_)
    m["cak"] = np.ascontiguousarray(np.asarray(inp["cache_a_k"][0, bsl], dtype=f)[:, ::-1].reshape(4, 512, 256))
    m["cav"] = np.ascontiguousarray(np.asarray(inp["cache_a_v"][0, bsl], dtype=f)[:, ::-1].reshape(4, 512, 256))
    m["cbk"] = np.ascontiguousarray(np.asarray(inp["cache_b_k"][0, bsl], dtype=f)[:, ::-1].reshape(4, PAST, 512))
    m["cbv"] = np.ascontiguousarray(np.asarray(inp["cache_b_v"][0, bsl], dtype=f)[:, ::-1].reshape(4, PAST, 512))
    m["cmk"] = np.ascontiguousarray(np.asarray(inp["cache_mem_k"][0, bsl], dtype=f).reshape(4, 256, 256))
    m["cmv"] = np.ascontiguousarray(np.asarray(inp["cache_mem_v"][0, bsl], dtype=f).reshape(4, 256, 256))
    st = np.asarray(inp["state_ffn_conv"][0, bsl], dtype=f)
    m["stT"] = np.ascontiguousarray(st[:, ::-1, :].reshape(4, 2, 2 * NFT, 128).transpose(3, 2, 0, 1))
    return m


def _shared_inputs(inp):
    f = np.float32
    m = {}
    tab = np.asarray(inp["rel_bias"][0], dtype=f)
    jq = np.arange(128)[:, None]
    jk = np.arange(640)[None, :]
    m["bias_p"] = np.ascontiguousarray(tab[:, np.clip(jk - jq, -256, 256) + 256])
    jq = np.arange(16)[:, None]
    jk = np.arange(528)[None, :]
    rel_new = (15 - jq) - (15 - (jk - 512))
    rel_cache = (15 - jq) + 1 + jk
    rel = np.where(jk >= 512, rel_new, rel_cache)
    m["bias_s"] = np.ascontiguousarray(tab[:, np.clip(rel, -256, 256) + 256])
    m["w_in"] = np.ascontiguousarray(np.asarray(inp["w_in"][0], dtype=f))
    m["w_mem"] = np.ascontiguousarray(np.asarray(inp["w_mem_kv"][0], dtype=f))
    m["w_pa"] = np.ascontiguousarray(np.asarray(inp["w_pa"][0], dtype=f))
    m["w_pb"] = np.ascontiguousarray(np.asarray(inp["w_pb"][0], dtype=f))
    m["w_pm"] = np.ascontiguousarray(np.asarray(inp["w_pm"][0], dtype=f))
    m["w_gate"] = np.ascontiguousarray(np.asarray(inp["w_gate"][0], dtype=f))
    m["b_gateT"] = np.ascontiguousarray(np.asarray(inp["b_gate"][0], dtype=f).reshape(24, 128).T)
    m["w_o"] = np.ascontiguousarray(np.asarray(inp["w_o"][0], dtype=f))
    m["ln1_g"] = np.ascontiguousarray(np.broadcast_to(np.asarray(inp["ln1_g"], dtype=f).reshape(1, D), (128, D)))
    m["ln1_b"] = np.ascontiguousarray(np.broadcast_to(np.asarray(inp["ln1_b"], dtype=f).reshape(1, D), (128, D)))
    m["w_up"] = np.ascontiguousarray(np.asarray(inp["w_up"][0], dtype=f))
    m["conv_wT"] = np.ascontiguousarray(np.asarray(inp["conv_w"][0], dtype=f).reshape(3, 2 * NFT, 128).transpose(2, 1, 0))
    m["conv_bT"] = np.ascontiguousarray(np.asarray(inp["conv_b"][0], dtype=f).reshape(2 * NFT, 128).T)
    m["w_down"] = np.ascontiguousarray(np.asarray(inp["w_down"][0], dtype=f))
    m["ln2_g"] = np.ascontiguousarray(np.broadcast_to(np.asarray(inp["ln2_g"], dtype=f).reshape(1, D), (128, D)))
    m["ln2_b"] = np.ascontiguousarray(np.broadcast_to(np.asarray(inp["ln2_b"], dtype=f).reshape(1, D), (128, D)))
    return m


def kernel(**inp):
    nc = build()
    shared = _shared_inputs(inp)
    in_maps = []
    for c in range(8):
        m = dict(shared)
        m.update(_prep_inputs(inp, c))
        in_maps.append(m)
    res = run_bass_kernel_spmd(nc, in_maps, core_ids=list(range(8)))
    R = res.results
    f = np.float32
    yp = np.stack([np.asarray(R[c]["yp_d"], dtype=f)[::-1] for c in range(8)])
    ys = np.concatenate([np.asarray(R[c]["ys_d"], dtype=f).reshape(4, 16, D)[:, ::-1] for c in range(8)])
    pak = np.stack([np.asarray(R[c]["pak_d"], dtype=f)[::-1].reshape(512, 4, 64) for c in range(8)])[None]
    pav = np.stack([np.asarray(R[c]["pav_d"], dtype=f)[::-1].reshape(512, 4, 64) for c in range(8)])[None]
    pbk = np.stack([np.asarray(R[c]["pbk_d"], dtype=f)[::-1].reshape(S, 8, 64) for c in range(8)])[None]
    pbv = np.stack([np.asarray(R[c]["pbv_d"], dtype=f)[:S][::-1].reshape(S, 8, 64) for c in range(8)])[None]
    pmk = np.stack([np.asarray(R[c]["pmkv_d"], dtype=f)[:, 0:256].reshape(256, 4, 64) for c in range(8)])[None]
    pmv = np.stack([np.asarray(R[c]["pmkv_d"], dtype=f)[:, 256:512].reshape(256, 4, 64) for c in range(8)])[None]
    pconv = np.stack([np.asarray(R[c]["pconv_d"], dtype=f).transpose(2, 1, 0).reshape(2, 2 * DFF)[::-1]
                      for c in range(8)])[None]
    sak = np.concatenate([np.asarray(R[c]["sak_d"], dtype=f).reshape(4, 16, 4, 64)[:, ::-1] for c in range(8)])[None]
    sav = np.concatenate([np.asarray(R[c]["sav_d"], dtype=f).reshape(4, 16, 4, 64)[:, ::-1] for c in range(8)])[None]
    sbk = np.concatenate([np.asarray(R[c]["sbk_d"], dtype=f).reshape(4, 16, 8, 64)[:, ::-1] for c in range(8)])[None]
    sbv = np.concatenate([np.asarray(R[c]["sbv_d"], dtype=f).reshape(4, 16, 8, 64)[:, ::-1] for c in range(8)])[None]
    sconv = np.concatenate([np.asarray(R[c]["sconv_d"], dtype=f).transpose(2, 3, 1, 0).reshape(4, 2, 2 * DFF)[:, ::-1]
                            for c in range(8)])[None]
    outs = (yp, ys, pak, pav, pbk, pbv, pmk, pmv, pconv, sak, sav, sbk, sbv, sconv)
    return tuple(np.ascontiguousarray(o, dtype=f) for o in outs)
```
